# Optimizing a Trainium2 kernel written in Bass

```python
import jax, jax.numpy as jnp
from jax import lax
import numpy as np

D_MODEL = 1024
BATCH = 1
SEQ = 16384
DEPTH = 1
DEC_BATCH = 8
DEC_SEQ = 8192
PAST_LEN = 128

MIX_WIDTH = D_MODEL
FOURIER_WIDTH = MIX_WIDTH // 2
FOURIER_GROUP_DIM = 128
FOURIER_GROUPS = FOURIER_WIDTH // FOURIER_GROUP_DIM
SGU_WIDTH = MIX_WIDTH - FOURIER_WIDTH
SGU_HEAD_DIM = 128
SGU_HEADS = SGU_WIDTH // SGU_HEAD_DIM
CHUNK = 128
IN_COLS = FOURIER_WIDTH + 2 * SGU_WIDTH
D_FF = 2816
CONV_WIDTH = 3
EPS = 1e-6

kernel_name = "fnet_gmlp_hybrid_encoder"


def _rmsnorm(x, g):
    xf = x.astype(jnp.float32)
    y = xf * lax.rsqrt(jnp.mean(xf * xf, axis=-1, keepdims=True) + EPS)
    return (y * g.astype(jnp.float32)).astype(x.dtype)


def _token_mixer(h, w_in, sgu_norm, w_spatial, b_spatial, w_out):
    b, s, _ = h.shape
    z = h @ w_in
    zf = z[..., :FOURIER_WIDTH].reshape(b, s, FOURIER_GROUPS, FOURIER_GROUP_DIM)
    f = jnp.fft.fft2(zf.astype(jnp.float32), axes=(1, 3), norm="ortho").real
    f = f.astype(h.dtype).reshape(b, s, FOURIER_WIDTH)
    zs = jax.nn.gelu(z[..., FOURIER_WIDTH:])
    u = zs[..., :SGU_WIDTH]
    v = zs[..., SGU_WIDTH:].reshape(b, s, SGU_HEADS, SGU_HEAD_DIM)
    v = _rmsnorm(v, sgu_norm)
    v = v.reshape(b, s // CHUNK, CHUNK, SGU_HEADS, SGU_HEAD_DIM)
    sv = jnp.einsum('hpq,bnqhc->bnphc', w_spatial, v) + b_spatial.T[None, None, :, :, None]
    g = u * sv.reshape(b, s, SGU_WIDTH)
    return jnp.concatenate([f, g], axis=-1) @ w_out


def _channel_mixer(h, w_up, conv_w, conv_b, w_down):
    a = h @ w_up
    ap = jnp.pad(a, ((0, 0), (1, 1), (0, 0)))
    a = ap[:, :-2] * conv_w[0] + ap[:, 1:-1] * conv_w[1] + ap[:, 2:] * conv_w[2] + conv_b
    gate, up = a[..., :D_FF], a[..., D_FF:]
    return (jax.nn.gelu(gate) * up) @ w_down


def _trunk(x, pre_mix_norm, w_in, sgu_norm, w_spatial, b_spatial, w_out, post_mix_norm,
           pre_ffn_norm, w_up, conv_w, conv_b, w_down, post_ffn_norm):
    for l in range(DEPTH):
        m = _token_mixer(_rmsnorm(x, pre_mix_norm[l]), w_in[l], sgu_norm[l],
                         w_spatial[l], b_spatial[l], w_out[l])
        x = x + _rmsnorm(m, post_mix_norm[l])
        c = _channel_mixer(_rmsnorm(x, pre_ffn_norm[l]), w_up[l], conv_w[l], conv_b[l], w_down[l])
        x = x + _rmsnorm(c, post_ffn_norm[l])
    return x


def setup_inputs(seed: int = 0) -> dict:
    key = jax.random.key(seed)
    ks = jax.random.split(key, 16)
    f32 = jnp.float32
    nrm = lambda k, shp, sc: jax.random.normal(k, shp, f32) * sc
    gain = lambda k, shp: 1.0 + 0.05 * jax.random.normal(k, shp, f32)
    return {
        "x_prompt": jax.random.normal(ks[0], (BATCH, SEQ, D_MODEL), f32),
        "x_sample": jax.random.normal(ks[1], (DEC_BATCH, DEC_SEQ, D_MODEL), f32),
        "pre_mix_norm": gain(ks[2], (DEPTH, D_MODEL)),
        "w_in": nrm(ks[3], (DEPTH, D_MODEL, IN_COLS), D_MODEL ** -0.5),
        "sgu_norm": gain(ks[4], (DEPTH, SGU_HEADS, SGU_HEAD_DIM)),
        "w_spatial": nrm(ks[5], (DEPTH, SGU_HEADS, CHUNK, CHUNK), CHUNK ** -0.5),
        "b_spatial": nrm(ks[6], (DEPTH, SGU_HEADS, CHUNK), 0.02),
        "w_out": nrm(ks[7], (DEPTH, MIX_WIDTH, D_MODEL), MIX_WIDTH ** -0.5),
        "post_mix_norm": gain(ks[8], (DEPTH, D_MODEL)),
        "pre_ffn_norm": gain(ks[9], (DEPTH, D_MODEL)),
        "w_up": nrm(ks[10], (DEPTH, D_MODEL, 2 * D_FF), D_MODEL ** -0.5),
        "conv_w": nrm(ks[11], (DEPTH, CONV_WIDTH, 2 * D_FF), CONV_WIDTH ** -0.5),
        "conv_b": nrm(ks[12], (DEPTH, 2 * D_FF), 0.02),
        "w_down": nrm(ks[13], (DEPTH, D_FF, D_MODEL), D_FF ** -0.5),
        "post_ffn_norm": gain(ks[14], (DEPTH, D_MODEL)),
    }


def reference(x_prompt, x_sample, pre_mix_norm, w_in, sgu_norm, w_spatial, b_spatial, w_out,
              post_mix_norm, pre_ffn_norm, w_up, conv_w, conv_b, w_down, post_ffn_norm):
    y_prompt = _trunk(x_prompt, pre_mix_norm, w_in, sgu_norm, w_spatial, b_spatial, w_out,
                      post_mix_norm, pre_ffn_norm, w_up, conv_w, conv_b, w_down, post_ffn_norm)
    y_sample = _trunk(x_sample, pre_mix_norm, w_in, sgu_norm, w_spatial, b_spatial, w_out,
                      post_mix_norm, pre_ffn_norm, w_up, conv_w, conv_b, w_down, post_ffn_norm)
    return (y_prompt, y_sample)
```

```python
from contextlib import ExitStack
import numpy as np
import ml_dtypes
import concourse.bass as bass
import concourse.mybir as mybir
from concourse.bass_utils import run_bass_kernel_spmd

F32 = mybir.dt.float32
BF16 = mybir.dt.bfloat16
AF = mybir.ActivationFunctionType
ALU = mybir.AluOpType

D = 1024
FW = 512
INC = 1536
DFF = 2816
NJ = DFF // 128
EPS = 1e-6
NCORES = 8
KH = 65

PE, ACT, DVE, POOL, SP = "tensor", "scalar", "vector", "gpsimd", "sync"
ENGS = (PE, ACT, DVE, POOL, SP)
DMA_SLOTS = 8
SAME_ENGINE_SYNC = True


class Buf:
    __slots__ = ("name", "w", "r", "excl")

    def __init__(self, name="", excl=False):
        self.name = name
        self.w = None
        self.r = {}
        self.excl = excl


class Prog:
    def __init__(self, nc):
        self.nc = nc
        self.ops = {e: [] for e in ENGS}
        self.cnt = {e: 0 for e in ENGS}
        self.known = {e: {} for e in ENGS}
        self.dma_n = {e: 0 for e in ENGS}
        self.semkeys = set()

    def _deps(self, eng, reads, writes, extra=(), own=None):
        need = {}

        def add(t):
            if t is None:
                return
            k, v = t
            if need.get(k, 0) < v:
                need[k] = v

        for b in reads:
            add(b.w)
            if b.excl:
                for k, v in b.r.items():
                    if k != own:
                        add((k, v))
        for b in writes:
            add(b.w)
            for k, v in b.r.items():
                add((k, v))
        for t in extra:
            add(t)
        out = []
        kn = self.known[eng]
        for k, v in need.items():
            if (not SAME_ENGINE_SYNC) and k == ("c", eng):
                continue
            if kn.get(k, 0) >= v:
                continue
            kn[k] = v
            out.append((k, v))
        return out

    def _mark(self, tok, reads, writes):
        k, v = tok
        for b in reads:
            if b.r.get(k, 0) < v:
                b.r[k] = v
        for b in writes:
            b.w = tok
            b.r = {}

    def op(self, eng, fn, reads=(), writes=()):
        waits = self._deps(eng, reads, writes, own=("c", eng))
        self.cnt[eng] += 1
        key = ("c", eng)
        self.semkeys.add(key)
        tok = (key, self.cnt[eng])
        self.ops[eng].append((waits, fn, key, 1))
        self._mark(tok, reads, writes)
        return tok

    def dma(self, q, fn, reads=(), writes=()):
        n = self.dma_n[q]
        self.dma_n[q] += 1
        slot = n % DMA_SLOTS
        key = ("d", q, slot)
        self.semkeys.add(key)
        val = 16 * (n // DMA_SLOTS + 1)
        extra = [(key, val - 16)] if val > 16 else []
        waits = self._deps(q, reads, writes, extra)
        tok = (key, val)
        self.ops[q].append((waits, fn, key, 16))
        self._mark(tok, reads, writes)
        return tok

    def barrier(self, engs=ENGS):
        allt = []
        for e in ENGS:
            if self.cnt[e]:
                allt.append((("c", e), self.cnt[e]))
            n = self.dma_n[e]
            for s in range(min(n, DMA_SLOTS)):
                cntv = (n - 1 - s) // DMA_SLOTS + 1
                allt.append((("d", e, s), 16 * cntv))
        for e in engs:
            waits = self._deps(e, (), (), allt)
            if waits:
                self.ops[e].append((waits, None, None, 0))

    def emit(self):
        nc = self.nc
        with ExitStack() as es:
            sems = {}
            for k in sorted(self.semkeys, key=str):
                sems[k] = es.enter_context(nc.semaphore("s_" + "_".join(str(x) for x in k)))
            block = es.enter_context(nc.Block())

            def make(eng):
                def body(e):
                    for waits, fn, key, inc in self.ops[eng]:
                        for k, v in waits:
                            e.wait_ge(sems[k], v)
                        if fn is not None:
                            fn(e).then_inc(sems[key], inc)
                return body

            for eng in ENGS:
                if self.ops[eng]:
                    getattr(block, eng)(make(eng))


class Arena:
    def __init__(self, nc, name, nbytes):
        self.t = nc.alloc_sbuf_tensor(name, [128, nbytes // 2], BF16)
        self.ap = self.t[:]
        self.off = 0
        self.cap = nbytes

    def mark(self):
        return self.off

    def reset(self, m):
        self.off = m

    def alloc(self, n_elems, dtype=BF16):
        esz = 4 if dtype == F32 else 2
        nb = (n_elems * esz + 63) // 64 * 64
        assert self.off + nb <= self.cap, f"arena overflow {self.off}+{nb}>{self.cap}"
        a = self.ap[:, self.off // 2:(self.off + nb) // 2]
        self.off += nb
        if dtype == F32:
            a = a.bitcast(F32)
        return a[:, 0:n_elems]


class Cfg:
    def __init__(self, S_s=8192, S_p=16384, debug=False):
        self.S_s = S_s
        self.S_p = S_p
        self.OWN = S_p // NCORES
        self.EXT = self.OWN + 256
        self.NF_s = S_s // 128
        self.NF_p = S_p // 128
        self.NKS_p = self.OWN // 128 + 2
        self.debug = debug
        assert S_s % 512 == 0 and self.OWN % 512 == 0


def build_program(cfg):
    nc = bass.Bass("TRN2", target_bir_lowering=False)
    P = Prog(nc)
    dbg = cfg.debug

    def din(name, shape, dt=F32):
        return nc.dram_tensor(name, list(shape), dt, kind="ExternalInput").ap()

    def dout(name, shape, dt=F32):
        return nc.dram_tensor(name, list(shape), dt, kind="ExternalOutput").ap()

    def dscr(name, shape, dt):
        kind = "ExternalOutput" if dbg else "Internal"
        return nc.dram_tensor(name, list(shape), dt, kind=kind).ap()

    S_s, S_p, OWN, EXT = cfg.S_s, cfg.S_p, cfg.OWN, cfg.EXT
    xs = din("xs", [S_s, D])
    xpf = din("xpf", [S_p, D])
    xst = din("xst", [S_s, D])
    xpo = din("xpo", [EXT, D])
    maskd = din("mask", [128, 2])
    g_pre = din("g_pre", [2, 8, 128])
    w_in = din("w_in", [D, INC])
    sgain = din("sgain", [512])
    w_sp = din("w_sp", [4, 128, 128])
    b_sp = din("b_sp", [512])
    w_out = din("w_out", [D, D])
    g_pm = din("g_pm", [D])
    w_up = din("w_up", [D, 2 * DFF])
    conv_w = din("conv_w", [132, 128])
    conv_b = din("conv_b", [44, 128])
    w_down = din("w_down", [DFF, D])
    g_pf = din("g_pf", [D])
    identb_d = din("identb", [128, 128], BF16)
    identf_d = din("identf", [128, 128])
    tabA_s = din("tabA_s", [128, cfg.NF_s * 2 * KH], BF16)
    tabA_p = din("tabA_p", [128, cfg.NF_p * 2 * KH], BF16)
    tabC_s = din("tabC_s", [cfg.NF_s, 2, 2 * cfg.NF_s], BF16)
    tabC_p = din("tabC_p", [cfg.NF_p, 2, 2 * cfg.NKS_p], BF16)
    tabM_s = din("tabM_s", [cfg.NF_s, 2, 2 * cfg.NF_s], BF16)
    tabM_p = din("tabM_p", [cfg.NF_p, 2, 2 * cfg.NKS_p], BF16)
    cs_s = din("cs_s", [128, 2, 128], BF16)
    cs_p = din("cs_p", [128, 2, 128], BF16)
    ys = dout("ys", [S_s, D])
    yp = dout("yp", [OWN, D])
    A_s = dscr("A_s", [KH, cfg.NF_s, 2 * FW], BF16)
    A_p = dscr("A_p", [KH, cfg.NF_p, 2 * FW], BF16)
    FT_s = dscr("FT_s", [4, 128, S_s], BF16)
    FT_p = dscr("FT_p", [4, 128, EXT], BF16)
    X1_s = dscr("X1_s", [S_s + 2, D], F32)
    X1_p = dscr("X1_p", [EXT, D], F32)

    banks = [nc.alloc_psum_tensor(f"bank{i}", [128, 512], F32) for i in range(8)]
    bkf = [b[:] for b in banks]
    bkb = [b[:].bitcast(BF16) for b in banks]
    BK = [Buf(f"bank{i}", excl=True) for i in range(8)]

    CON = Arena(nc, "con", 16896)
    REG = Arena(nc, "reg", 132 * 1024)
    WRK = Arena(nc, "wrk", 58 * 1024)

    identb = CON.alloc(128); identf = CON.alloc(128, F32)
    gcol = CON.alloc(16, F32)
    cwT = CON.alloc(132, F32)
    cbT = CON.alloc(44, F32)
    maskt = CON.alloc(2, F32)
    epst = CON.alloc(1, F32)
    mhalf = CON.alloc(4, F32)
    bcol = CON.alloc(4, F32)
    gpm_bc = CON.alloc(D, F32); gpf_bc = CON.alloc(D, F32)
    sgain_bc = CON.alloc(512, F32); bsp_bc = CON.alloc(512, F32)
    WsT = CON.alloc(512)
    cs128 = [CON.alloc(256), CON.alloc(256)]
    B_con = Buf("con")

    def ld(dst, src, q=SP, bufs=(B_con,)):
        P.dma(q, lambda e: e.dma_start(out=dst, in_=src), writes=list(bufs))

    ld(identb, identb_d[:, :]); ld(identf, identf_d[:, :]); ld(maskt, maskd[:, :])
    ld(gpm_bc, g_pm.partition_broadcast(128)); ld(gpf_bc, g_pf.partition_broadcast(128))
    ld(sgain_bc, sgain.partition_broadcast(128)); ld(bsp_bc, b_sp.partition_broadcast(128))
    ld(cs128[0], cs_s.rearrange("c t m -> c (t m)")); ld(cs128[1], cs_p.rearrange("c t m -> c (t m)"))
    P.op(DVE, lambda e: e.memset(epst, EPS), writes=[B_con])
    P.op(DVE, lambda e: e.memset(mhalf, -0.5), writes=[B_con])
    m0 = WRK.mark()
    zrow = WRK.alloc(D, F32)
    P.op(DVE, lambda e: e.memset(zrow, 0.0), writes=[B_con])
    B_x1s = Buf("x1s_dram")
    P.dma(SP, lambda e: e.dma_start(out=X1_s[0:1, :], in_=zrow[0:1, :]), reads=[B_con], writes=[B_x1s])
    P.dma(SP, lambda e: e.dma_start(out=X1_s[S_s + 1:S_s + 2, :], in_=zrow[0:1, :]), reads=[B_con], writes=[B_x1s])

    st_rows = WRK.alloc(128, F32); st_rows2 = WRK.alloc(128, F32); st_rows3 = WRK.alloc(128, F32); st_rows4 = WRK.alloc(128, F32); st_rows5 = WRK.alloc(128, F32)
    B_st = Buf("st")
    ld(st_rows[0:16, :], g_pre.rearrange("a k p -> (a k) p"), bufs=(B_st,))
    ld(st_rows2[0:128, :], conv_w[0:128, :], bufs=(B_st,))
    ld(st_rows3[0:4, :], conv_w[128:132, :], bufs=(B_st,))
    ld(st_rows4[0:44, :], conv_b[:, :], bufs=(B_st,))
    ld(st_rows5[0:4, :], b_sp.rearrange("(h p) -> h p", h=4), bufs=(B_st,))

    def trf(e):
        e.transpose(out=bkf[0][:, 0:16], in_=st_rows[0:16, :], identity=identf[0:16, 0:16])
        e.transpose(out=bkf[0][:, 16:144], in_=st_rows2[0:128, :], identity=identf[:, :])
        e.transpose(out=bkf[0][:, 144:148], in_=st_rows3[0:4, :], identity=identf[0:4, 0:4])
        e.transpose(out=bkf[0][:, 148:192], in_=st_rows4[0:44, :], identity=identf[0:44, 0:44])
        return e.transpose(out=bkf[0][:, 192:196], in_=st_rows5[0:4, :], identity=identf[0:4, 0:4])
    P.op(PE, trf, reads=[B_st, B_con], writes=[BK[0]])
    P.op(DVE, lambda e: e.tensor_copy(out=gcol, in_=bkf[0][:, 0:16]), reads=[BK[0]], writes=[B_con])
    P.op(DVE, lambda e: e.tensor_copy(out=cwT, in_=bkf[0][:, 16:148]), reads=[BK[0]], writes=[B_con])
    P.op(DVE, lambda e: e.tensor_copy(out=cbT, in_=bkf[0][:, 148:192]), reads=[BK[0]], writes=[B_con])
    P.op(DVE, lambda e: e.tensor_copy(out=bcol, in_=bkf[0][:, 192:196]), reads=[BK[0]], writes=[B_con])
    wsp_f = WRK.alloc(512, F32); wsp_b = WRK.alloc(512)
    ld(wsp_f, w_sp.rearrange("h p q -> p h q"), bufs=(B_st,))
    P.op(DVE, lambda e: e.tensor_copy(out=wsp_b, in_=wsp_f), reads=[B_st], writes=[B_st])

    def trw(e):
        ins = None
        for hd in range(4):
            ins = e.transpose(out=bkb[1][:, hd * 128:(hd + 1) * 128], in_=wsp_b[:, hd * 128:(hd + 1) * 128], identity=identb)
        return ins
    P.op(PE, trw, reads=[B_st, B_con], writes=[BK[1]])
    P.op(DVE, lambda e: e.tensor_copy(out=WsT, in_=bkb[1][:, 0:512]), reads=[BK[1]], writes=[B_con])
    P.barrier()
    WRK.reset(m0)

    def load_weight(dst3, src, kchunks, ncols, scale_col0=None, stage_cols=1536):
        mk = WRK.mark()
        NSTG = 6
        stg = [WRK.alloc(stage_cols, F32) for _ in range(NSTG)]
        B_stg = [Buf() for _ in range(NSTG)]
        B_w = Buf("w")
        i = 0
        for k in range(kchunks):
            for c0 in range(0, ncols, stage_cols):
                c1 = min(ncols, c0 + stage_cols)
                s = stg[i % NSTG]; bs = B_stg[i % NSTG]
                P.dma(SP, (lambda e, s=s, k=k, c0=c0, c1=c1: e.dma_start(out=s[:, 0:c1 - c0], in_=src[k * 128:(k + 1) * 128, c0:c1])), writes=[bs])
                dsl = dst3[:, k * ncols + c0:k * ncols + c1]
                eng = DVE if i % 2 == 0 else ACT
                if scale_col0 is None:
                    if eng == DVE:
                        P.op(DVE, (lambda e, s=s, dsl=dsl, n=c1 - c0: e.tensor_copy(out=dsl, in_=s[:, 0:n])), reads=[bs], writes=[B_w])
                    else:
                        P.op(ACT, (lambda e, s=s, dsl=dsl, n=c1 - c0: e.activation(out=dsl, in_=s[:, 0:n], func=AF.Copy)), reads=[bs], writes=[B_w])
                else:
                    sc = gcol[:, scale_col0 + k:scale_col0 + k + 1]
                    if eng == DVE:
                        P.op(DVE, (lambda e, s=s, dsl=dsl, n=c1 - c0, sc=sc: e.tensor_scalar(out=dsl, in0=s[:, 0:n], scalar1=sc, scalar2=None, op0=ALU.mult)), reads=[bs, B_con], writes=[B_w])
                    else:
                        P.op(ACT, (lambda e, s=s, dsl=dsl, n=c1 - c0, sc=sc: e.activation(out=dsl, in_=s[:, 0:n], func=AF.Copy, scale=sc)), reads=[bs, B_con], writes=[B_w])
                i += 1
        P.barrier()
        WRK.reset(mk)

    w_in_b = REG.alloc(8 * INC)
    w_out_b = REG.alloc(8 * D)
    load_weight(w_in_b, w_in, 8, INC, scale_col0=0)
    load_weight(w_out_b, w_out, 8, D)
    regA_mark = REG.mark()

    def w_in_sl(k, c0, c1):
        return w_in_b[:, k * INC + c0:k * INC + c1]

    def w_out_sl(k, c0, c1):
        return w_out_b[:, k * D + c0:k * D + c1]

    def rstd_pool(ss, rstd, n, np_, inv_n, Bss):
        P.op(POOL, lambda e: e.tensor_scalar(out=rstd[0:np_, 0:n], in0=ss[0:np_, 0:n], scalar1=inv_n, scalar2=EPS, op0=ALU.mult, op1=ALU.add), reads=[Bss], writes=[Bss])
        P.op(POOL, lambda e: e.tensor_tensor(out=rstd[0:np_, 0:n], in0=rstd[0:np_, 0:n], in1=mhalf[0:np_, 0:n], op=ALU.pow), reads=[Bss, B_con], writes=[Bss])

    def norm_to_bf16(x_ap, np_, junk, ss, rstd, hb, Bx, Bss, Bh, Bjunk, scale_eng=DVE):
        P.op(ACT, lambda e: e.activation(out=junk[0:np_, :], in_=x_ap, func=AF.Square, accum_out=ss[0:np_, :]), reads=[Bx], writes=(list(Bjunk) if isinstance(Bjunk, (list, tuple)) else [Bjunk]) + [Bss])
        rstd_pool(ss, rstd, 1, np_, 1.0 / D, Bss)
        if scale_eng == DVE:
            P.op(DVE, lambda e: e.tensor_scalar(out=hb[0:np_, :], in0=x_ap, scalar1=rstd[0:np_, :], scalar2=None, op0=ALU.mult), reads=[Bx, Bss], writes=[Bh])
        else:
            P.op(ACT, lambda e: e.activation(out=hb[0:np_, :], in_=x_ap, func=AF.Copy, scale=rstd[0:np_, :]), reads=[Bx, Bss], writes=[Bh])

    def run_pipeline(n, stages, group=1):
        ns = len(stages)
        ng = (n + group - 1) // group
        for it in range(ng + ns - 1):
            for k in reversed(range(ns)):
                g = it - k
                if 0 <= g < ng:
                    for i in range(g * group, min(n, (g + 1) * group)):
                        stages[k](i)

    def phase1(xsrc, S, NF, tabA, Ascr):
        mk_r, mk_w = REG.mark(), WRK.mark()
        NX = 6
        xt = [REG.alloc(D, F32) for _ in range(NX)]
        junk = REG.alloc(D)
        hb = [REG.alloc(D) for _ in range(2)]
        hT = [REG.alloc(D) for _ in range(2)]
        zt = [REG.alloc(FW) for _ in range(2)]
        tball = REG.alloc(NF * 2 * KH)
        B_tb = Buf()
        P.dma(SP, lambda e: e.dma_start(out=tball, in_=tabA[:, :]), writes=[B_tb])
        NA = 4
        At = [REG.alloc(2 * FW) for _ in range(NA)]
        st = [WRK.alloc(2, F32) for _ in range(4)]
        Bxt = [Buf() for _ in range(NX)]; Bjunk = Buf(); Bhb = [Buf(), Buf()]
        BhT = [Buf(), Buf()]; Bzt = [Buf(), Buf()]; Btb = None
        BAt = [Buf() for _ in range(NA)]; Bst = [Buf() for _ in range(4)]
        B_A = Buf("Ascr")
        xv = xsrc.rearrange("(f i) d -> f i d", i=128)

        def s0(i):
            P.dma(SP, (lambda e: e.dma_start(out=xt[i % NX], in_=xv[i])), writes=[Bxt[i % NX]])

        def s1(i):
            s_ = st[i % 4]
            P.op(ACT, (lambda e: e.activation(out=junk, in_=xt[i % NX], func=AF.Square, accum_out=s_[:, 0:1])), reads=[Bxt[i % NX]], writes=[Bjunk, Bst[i % 4]])

        def s1b(i):
            s_ = st[i % 4]
            rstd_pool(s_[:, 0:1], s_[:, 1:2], 1, 128, 1.0 / D, Bst[i % 4])

        def s1c(i):
            s_ = st[i % 4]
            P.op(DVE, (lambda e: e.tensor_scalar(out=hb[i % 2], in0=xt[i % NX], scalar1=s_[:, 1:2], scalar2=None, op0=ALU.mult)), reads=[Bxt[i % NX], Bst[i % 4]], writes=[Bhb[i % 2]])

        def s2(i):
            b, pb = i % 2, i % 2
            def tr(e):
                ins = None
                for k in range(8):
                    ins = e.transpose(out=bkb[pb][:, k * 128:(k + 1) * 128], in_=hb[b][:, k * 128:(k + 1) * 128], identity=identb)
                return ins
            P.op(PE, tr, reads=[Bhb[b], B_con], writes=[BK[pb]])

        def s3(i):
            b, pb = i % 2, i % 2
            P.op(ACT, (lambda e: e.activation(out=hT[b], in_=bkb[pb], func=AF.Copy)), reads=[BK[pb]], writes=[BhT[b]])

        def s4(i):
            b, zb = i % 2, 2 + i % 2
            def mmz(e):
                ins = None
                for k in range(8):
                    ins = e.matmul(bkf[zb], lhsT=hT[b][:, k * 128:(k + 1) * 128], rhs=w_in_sl(k, 0, FW), start=(k == 0), stop=(k == 7))
                return ins
            P.op(PE, mmz, reads=[BhT[b]], writes=[BK[zb]])

        def s5(i):
            b, zb = i % 2, 2 + i % 2
            P.op(DVE, (lambda e: e.tensor_copy(out=zt[b], in_=bkf[zb])), reads=[BK[zb]], writes=[Bzt[b]])

        def s6(i):
            b, ab = i % 2, 4 + 2 * (i % 2)
            tc0 = i * 2 * KH
            P.op(PE, (lambda e: e.matmul(bkf[ab][0:KH, :], lhsT=tball[:, tc0:tc0 + KH], rhs=zt[b], start=True, stop=True)), reads=[B_tb, Bzt[b]], writes=[BK[ab]])
            P.op(PE, (lambda e: e.matmul(bkf[ab + 1][0:KH, :], lhsT=tball[:, tc0 + KH:tc0 + 2 * KH], rhs=zt[b], start=True, stop=True)), reads=[B_tb, Bzt[b]], writes=[BK[ab + 1]])

        def s7(i):
            b, ab = i % NA, 4 + 2 * (i % 2)
            At4 = At[b].rearrange("p (g t c) -> p g t c", g=4, t=2)
            P.op(ACT, (lambda e: e.activation(out=At4[0:KH, :, 0, :], in_=bkf[ab][0:KH, :].rearrange("p (g c) -> p g c", g=4), func=AF.Copy)), reads=[BK[ab]], writes=[BAt[b]])
            P.op(DVE, (lambda e: e.tensor_copy(out=At4[0:KH, :, 1, :], in_=bkf[ab + 1][0:KH, :].rearrange("p (g c) -> p g c", g=4))), reads=[BK[ab + 1]], writes=[BAt[b]])
            P.dma(POOL, (lambda e: e.dma_start(out=Ascr[:, i, :], in_=At[b][0:KH, :])), reads=[BAt[b]], writes=[B_A])

        run_pipeline(NF, [s0, s1, s1b, s1c, s2, s3, s4, s5, s6, s7], group=2)
        P.barrier()
        REG.reset(mk_r); WRK.reset(mk_w)

    def phase1b(NF, NKS, ntok, tabC, tabM, cs, Ascr, FTscr, NG):
        mk_r, mk_w = REG.mark(), WRK.mark()
        tC = REG.alloc(4 * NKS); tM = REG.alloc(4 * NKS)
        B_tC = Buf()
        P.dma(SP, lambda e: e.dma_start(out=tC[0:NF, :], in_=tabC.rearrange("s t n -> s (t n)")), writes=[B_tC])
        P.dma(SP, lambda e: e.dma_start(out=tM[0:NF, :], in_=tabM.rearrange("s t n -> s (t n)")), writes=[B_tC])
        XT = [REG.alloc(2 * ntok) for _ in range(NG)]
        B_XT = [Buf() for _ in range(NG)]
        NB = 3
        XW = NG * 256
        Ain = [REG.alloc(4 * XW) for _ in range(NB)]
        BAin = [Buf() for _ in range(NB)]
        FTo = [REG.alloc(512) for _ in range(2)]
        BFTo = [Buf() for _ in range(2)]
        B_FT = Buf("FTscr")
        Av = Ascr.rearrange("k s (gp x) -> k s gp x", gp=4 // NG)
        cpg = 8 * NKS
        gpb = max(1, min(NG, 512 // cpg))
        nbk = (NG + gpb - 1) // gpb
        assert nbk <= 2
        it = 0
        ld = 0
        ob = 0
        for gp in range(4 // NG):
            for kf0 in list(range(0, 64, 4)) + [64]:
                nk = 4 if kf0 < 64 else 1
                b = ld % NB
                ld += 1
                src = Av[kf0:kf0 + nk, :, gp, :].rearrange("k s x -> s k x")
                P.dma(SP, (lambda e, b=b, src=src, nk=nk: e.dma_start(out=Ain[b][0:NF, 0:nk * XW].rearrange("s (k x) -> s k x", k=nk), in_=src)), writes=[BAin[b]])
                for mirror in ((False, True) if kf0 < 64 else (False,)):
                    par = it % 2
                    it += 1
                    kls = [kl for kl in range(nk) if not (mirror and kf0 + kl == 0)]
                    tab = tM if mirror else tC
                    for bi in range(nbk):
                        pb = 2 * par + bi
                        gls = list(range(bi * gpb, min(NG, (bi + 1) * gpb)))
                        def mmc(e, b=b, pb=pb, gls=gls, kls=kls, tab=tab, mirror=mirror):
                            ins = None
                            for gi, gl in enumerate(gls):
                                for kl in kls:
                                    slot = (3 - kl) if mirror else kl
                                    o = bkf[pb][:, gi * cpg + slot * 2 * NKS:gi * cpg + (slot + 1) * 2 * NKS]
                                    base = kl * XW + gl * 256
                                    e.matmul(o, lhsT=Ain[b][0:NF, base:base + 128], rhs=tab[0:NF, 0:2 * NKS], start=True, stop=False)
                                    ins = e.matmul(o, lhsT=Ain[b][0:NF, base + 128:base + 256], rhs=tab[0:NF, 2 * NKS:4 * NKS], start=False, stop=True)
                            return ins
                        P.op(PE, mmc, reads=[BAin[b], B_tC], writes=[BK[pb]])
                        slots = sorted(((3 - kl) if mirror else kl) for kl in kls)
                        s0_, ns = slots[0], len(slots)
                        kdst0 = (128 - kf0 - 3 + s0_) if mirror else (kf0 + s0_)
                        for gi, gl in enumerate(gls):
                            src_v = bkf[pb][:, gi * cpg + s0_ * 2 * NKS:gi * cpg + (s0_ + ns) * 2 * NKS].rearrange("p (f r s) -> p f r s", f=ns, r=2)
                            dst_v = XT[gl].rearrange("p (r f s) -> p f r s", r=2, f=128, s=NKS)[:, kdst0:kdst0 + ns, :, :]
                            if bi == 0:
                                P.op(DVE, (lambda e, s_=src_v, d=dst_v: e.tensor_copy(out=d, in_=s_)), reads=[BK[pb]], writes=[B_XT[gl]])
                            else:
                                P.op(ACT, (lambda e, s_=src_v, d=dst_v: e.activation(out=d, in_=s_, func=AF.Copy)), reads=[BK[pb]], writes=[B_XT[gl]])
            for gl in range(NG):
                g = NG * gp + gl
                XTv = XT[gl].rearrange("p (r f s) -> p r s f", r=2, f=128, s=NKS)
                for c0 in range(0, ntok, 512):
                    c1 = min(ntok, c0 + 512)
                    n = c1 - c0
                    pb = 4 + ob % 2
                    fb = ob % 2
                    ob += 1
                    def mmch(e, c0=c0, n=n, pb=pb, XTv=XTv):
                        k0, nk_ = c0 // 128, n // 128
                        e.matmul(bkf[pb][:, 0:n], lhsT=cs[:, 0:128], rhs=XTv[:, 0, k0:k0 + nk_, :], start=True, stop=False)
                        return e.matmul(bkf[pb][:, 0:n], lhsT=cs[:, 128:256], rhs=XTv[:, 1, k0:k0 + nk_, :], start=False, stop=True)
                    P.op(PE, mmch, reads=[B_XT[gl], B_con], writes=[BK[pb]])
                    P.op(DVE, (lambda e, n=n, pb=pb, fb=fb: e.tensor_copy(out=FTo[fb][:, 0:n], in_=bkf[pb][:, 0:n])), reads=[BK[pb]], writes=[BFTo[fb]])
                    P.dma(POOL, (lambda e, g=g, c0=c0, n=n, fb=fb: e.dma_start(out=FTscr[g, :, c0:c0 + n], in_=FTo[fb][:, 0:n])), reads=[BFTo[fb]], writes=[B_FT])
        P.barrier()
        REG.reset(mk_r); WRK.reset(mk_w)

    phase1(xst, S_s, cfg.NF_s, tabA_s, A_s)
    phase1(xpf, S_p, cfg.NF_p, tabA_p, A_p)
    phase1b(cfg.NF_s, cfg.NF_s, S_s, tabC_s, tabM_s, cs128[0], A_s, FT_s, 2)
    phase1b(cfg.NF_p, cfg.NKS_p, EXT, tabC_p, tabM_p, cs128[1], A_p, FT_p, 4)

    def mixer_pass(units):
        mk_r, mk_w = REG.mark(), WRK.mark()
        tiles = []
        for (xsrc, r0, FTscr, X1, x1r0, nt, mcol) in units:
            for t in range(nt):
                tiles.append((xsrc, r0 + t * 128, FTscr, X1, x1r0 + t * 128, mcol))
        NT = len(tiles)
        def ring(n, elems, dt=BF16):
            return [REG.alloc(elems, dt) for _ in range(n)], [Buf() for _ in range(n)]
        xn, Bxn = ring(3, D, F32)
        xr, Bxr = ring(3, D, F32)
        junk = REG.alloc(D); Bjunk = Buf()
        junk2 = REG.alloc(D, F32); Bjunk2 = Buf()
        Btmp2 = [Buf() for _ in range(3)]
        hb, Bhb = ring(2, D)
        hT, BhT = ring(2, D)
        uf, Buf_ = ring(5, 512)
        vf, Bvf = ring(3, 512, F32)
        sq, Bsq = ring(2, 512, F32)
        vn, Bvn = ring(2, 512)
        gt, Bgt = ring(2, 512)
        gT, BgT = ring(2, 512)
        ft, Bft = ring(3, 512)
        tmp, Btmp = ring(3, D, F32)
        NS = 20
        st, Bst = [WRK.alloc(16, F32) for _ in range(NS)], [Buf() for _ in range(NS)]
        B_X1 = Buf("X1scr")
        bA = (0, 7)
        bU, bV, bS, bG, bO = 1, 2, 3, 4, (5, 6)

        def s_load(i):
            xsrc, r0, FTscr, X1, x1r0, mcol = tiles[i]
            P.dma(SP, (lambda e: e.dma_start(out=xn[i % 3], in_=xsrc[r0:r0 + 128, :])), writes=[Bxn[i % 3]])

        def s_sq(i):
            s_ = st[i % NS]
            P.op(ACT, (lambda e: e.activation(out=junk, in_=xn[i % 3], func=AF.Square, accum_out=s_[:, 0:1])), reads=[Bxn[i % 3]], writes=[Bjunk, Bst[i % NS]])
            rstd_pool(s_[:, 0:1], s_[:, 1:2], 1, 128, 1.0 / D, Bst[i % NS])

        def s_scale(i):
            s_ = st[i % NS]
            P.op(ACT, (lambda e: e.activation(out=hb[i % 2], in_=xn[i % 3], func=AF.Copy, scale=s_[:, 1:2])), reads=[Bxn[i % 3], Bst[i % NS]], writes=[Bhb[i % 2]])

        def s_tr(i):
            pb = bA[i % 2]
            def tr(e):
                ins = None
                for k in range(8):
                    ins = e.transpose(out=bkb[pb][:, k * 128:(k + 1) * 128], in_=hb[i % 2][:, k * 128:(k + 1) * 128], identity=identb)
                return ins
            P.op(PE, tr, reads=[Bhb[i % 2], B_con], writes=[BK[pb]])

        def s_evh(i):
            pb = bA[i % 2]
            P.op(ACT, (lambda e: e.activation(out=hT[i % 2], in_=bkb[pb], func=AF.Copy)), reads=[BK[pb]], writes=[BhT[i % 2]])

        def s_uv(i):
            for (bank, c0) in ((bU, FW), (bV, 1024)):
                def mm(e, bank=bank, c0=c0):
                    ins = None
                    for k in range(8):
                        ins = e.matmul(bkf[bank], lhsT=hT[i % 2][:, k * 128:(k + 1) * 128], rhs=w_in_sl(k, c0, c0 + 512), start=(k == 0), stop=(k == 7))
                    return ins
                P.op(PE, mm, reads=[BhT[i % 2]], writes=[BK[bank]])

        def s_gelu(i):
            P.op(ACT, (lambda e: e.activation(out=vf[i % 3], in_=bkf[bV], func=AF.Gelu_apprx_tanh)), reads=[BK[bV]], writes=[Bvf[i % 3]])
            P.op(ACT, (lambda e: e.activation(out=uf[i % 5], in_=bkf[bU], func=AF.Gelu_apprx_tanh)), reads=[BK[bU]], writes=[Buf_[i % 5]])

        def s_vsq(i):
            s_ = st[i % NS]
            P.op(ACT, (lambda e: e.activation(out=sq[i % 2], in_=vf[i % 3], func=AF.Square)), reads=[Bvf[i % 3]], writes=[Bsq[i % 2]])
            P.op(DVE, (lambda e: e.tensor_reduce(out=s_[:, 4:8], in_=sq[i % 2].rearrange("p (h n) -> p h n", h=4), axis=mybir.AxisListType.X, op=ALU.add)), reads=[Bsq[i % 2]], writes=[Bst[i % NS]])
            rstd_pool(s_[:, 4:8], s_[:, 8:12], 4, 128, 1.0 / 128, Bst[i % NS])

        def s_vn(i):
            s_ = st[i % NS]
            for hd in range(4):
                P.op(DVE, (lambda e, hd=hd: e.scalar_tensor_tensor(out=vn[i % 2][:, hd * 128:(hd + 1) * 128], in0=vf[i % 3][:, hd * 128:(hd + 1) * 128], scalar=s_[:, 8 + hd:9 + hd], in1=sgain_bc[:, hd * 128:(hd + 1) * 128], op0=ALU.mult, op1=ALU.mult)), reads=[Bvf[i % 3], Bst[i % NS], B_con], writes=[Bvn[i % 2]])

        def s_sp(i):
            def mms(e):
                ins = None
                for hd in range(4):
                    ins = e.matmul(bkf[bS][:, hd * 128:(hd + 1) * 128], lhsT=WsT[:, hd * 128:(hd + 1) * 128], rhs=vn[i % 2][:, hd * 128:(hd + 1) * 128], start=True, stop=True)
                return ins
            P.op(PE, mms, reads=[Bvn[i % 2], B_con], writes=[BK[bS]])

        def s_gate(i):
            for hd in range(4):
                P.op(DVE, (lambda e, hd=hd: e.scalar_tensor_tensor(out=gt[i % 2][:, hd * 128:(hd + 1) * 128], in0=bkf[bS][:, hd * 128:(hd + 1) * 128], scalar=bcol[:, hd:hd + 1], in1=uf[i % 5][:, hd * 128:(hd + 1) * 128], op0=ALU.add, op1=ALU.mult)), reads=[BK[bS], Buf_[i % 5], B_con], writes=[Bgt[i % 2]])

        def s_trg(i):
            xsrc, r0, FTscr, X1, x1r0, mcol = tiles[i]
            P.dma(SP, (lambda e: e.dma_start(out=ft[i % 3].rearrange("p (g n) -> p g n", g=4), in_=FTscr[:, :, r0:r0 + 128].rearrange("g m n -> m g n"))), writes=[Bft[i % 3]])
            def tr(e):
                ins = None
                for hd in range(4):
                    ins = e.transpose(out=bkb[bG][:, hd * 128:(hd + 1) * 128], in_=gt[i % 2][:, hd * 128:(hd + 1) * 128], identity=identb)
                return ins
            P.op(PE, tr, reads=[Bgt[i % 2], B_con], writes=[BK[bG]])

        def s_evg(i):
            P.op(ACT, (lambda e: e.activation(out=gT[i % 2], in_=bkb[bG][:, 0:512], func=AF.Copy)), reads=[BK[bG]], writes=[BgT[i % 2]])

        def s_wo(i):
            def mmo(e):
                ins = None
                for nh in range(2):
                    for k in range(8):
                        l = ft[i % 3][:, k * 128:(k + 1) * 128] if k < 4 else gT[i % 2][:, (k - 4) * 128:(k - 3) * 128]
                        ins = e.matmul(bkf[bO[nh]], lhsT=l, rhs=w_out_sl(k, nh * 512, (nh + 1) * 512), start=(k == 0), stop=(k == 7))
                return ins
            P.op(PE, mmo, reads=[Bft[i % 3], BgT[i % 2]], writes=[BK[bO[0]], BK[bO[1]]])

        def s_ocp(i):
            xsrc, r0, FTscr, X1, x1r0, mcol = tiles[i]
            ob = i % 3
            P.dma(SP, (lambda e: e.dma_start(out=xr[i % 3], in_=xsrc[r0:r0 + 128, :])), writes=[Bxr[i % 3]])
            P.op(ACT, (lambda e: e.activation(out=tmp[ob][:, 0:512], in_=bkf[bO[0]], func=AF.Copy)), reads=[BK[bO[0]]], writes=[Btmp[ob]])
            P.op(DVE, (lambda e: e.tensor_copy(out=tmp[ob][:, 512:1024], in_=bkf[bO[1]])), reads=[BK[bO[1]]], writes=[Btmp2[ob]])

        def s_osq(i):
            s_ = st[i % NS]
            ob = i % 3
            P.op(ACT, (lambda e: e.activation(out=junk2, in_=tmp[ob], func=AF.Square, accum_out=s_[:, 14:15])), reads=[Btmp[ob], Btmp2[ob]], writes=[Bjunk2, Bst[i % NS]])
            rstd_pool(s_[:, 14:15], s_[:, 15:16], 1, 128, 1.0 / D, Bst[i % NS])

        def s_out(i):
            xsrc, r0, FTscr, X1, x1r0, mcol = tiles[i]
            s_ = st[i % NS]
            ob = i % 3
            P.op(DVE, (lambda e: e.scalar_tensor_tensor(out=tmp[ob], in0=tmp[ob], scalar=s_[:, 15:16], in1=gpm_bc, op0=ALU.mult, op1=ALU.mult)), reads=[Btmp[ob], Btmp2[ob], Bst[i % NS], B_con], writes=[Btmp[ob], Btmp2[ob]])
            P.op(DVE, (lambda e: e.tensor_tensor(out=tmp[ob], in0=tmp[ob], in1=xr[i % 3], op=ALU.add)), reads=[Btmp[ob], Btmp2[ob], Bxr[i % 3]], writes=[Btmp[ob], Btmp2[ob]])
            if mcol is not None:
                P.op(DVE, (lambda e: e.tensor_scalar(out=tmp[ob], in0=tmp[ob], scalar1=maskt[:, mcol:mcol + 1], scalar2=None, op0=ALU.mult)), reads=[Btmp[ob], Btmp2[ob], B_con], writes=[Btmp[ob], Btmp2[ob]])
            P.dma(POOL, (lambda e: e.dma_start(out=X1[x1r0:x1r0 + 128, :], in_=tmp[ob])), reads=[Btmp[ob], Btmp2[ob]], writes=[B_X1])

        run_pipeline(NT, [s_load, s_sq, s_scale, s_tr, s_evh, s_uv, s_gelu, s_vsq, s_vn, s_sp, s_gate, s_trg, s_evg, s_wo, s_ocp, s_osq, s_out])
        P.barrier()
        REG.reset(mk_r); WRK.reset(mk_w)

    units = []
    for i in range(S_s // 512):
        units.append((xs, i * 512, FT_s, X1_s, 1 + i * 512, 4, None))
    units.append((xpo, 0, FT_p, X1_p, 0, 1, 0))
    for i in range(OWN // 512):
        units.append((xpo, 128 + i * 512, FT_p, X1_p, 128 + i * 512, 4, None))
    units.append((xpo, 128 + OWN, FT_p, X1_p, 128 + OWN, 1, 1))
    mixer_pass(units)

    REG.reset(0)
    w_up_b = REG.alloc(8 * 2 * DFF)
    w_dn_b = REG.alloc(NJ * D)
    load_weight(w_up_b, w_up, 8, 2 * DFF, scale_col0=8, stage_cols=1408)

    w_down_dram = w_down

    def ffn_pass(funits):
        xa = WRK.alloc(D, F32); Bxa = Buf()
        xr = [WRK.alloc(D, F32) for _ in range(2)]; Bxr = [Buf(), Buf()]
        otmp = WRK.alloc(D, F32); Bot = Buf(); Bot2 = Buf()
        mk_t = WRK.mark()
        trr = [WRK.alloc(256, F32) for _ in range(2)]; Btrr = [Buf(), Buf()]
        junk5 = WRK.ap[:, mk_t // 2:mk_t // 2 + 1024].bitcast(F32)
        hb = [WRK.alloc(D) for _ in range(2)]; Bhb = [Buf(), Buf()]
        h2T = WRK.alloc(8 * 514); Bh2T = [Buf() for _ in range(5)]
        pT = WRK.alloc(NJ * 512); BpT = [[Buf(), Buf()] for _ in range(NJ)]
        mk_j = WRK.mark()
        tg = [WRK.alloc(256, F32) for _ in range(2)]; Btg = [Buf(), Buf()]
        tu = [WRK.alloc(256, F32) for _ in range(2)]; Btu = [Buf(), Buf()]
        junkf = WRK.ap[:, mk_j // 2:mk_j // 2 + 2048].bitcast(F32)
        Bjunkf = Btg + Btu
        gg = [WRK.alloc(256) for _ in range(2)]; Bgg = [Buf(), Buf()]
        st = [WRK.alloc(8, F32) for _ in range(4)]; Bst = [Buf() for _ in range(4)]
        h2T3 = h2T.rearrange("p (k n) -> p k n", k=8)
        pT3 = pT.rearrange("p (j n) -> p j n", j=NJ)
        B_Y = Buf("yout")
        cnt = {"p": 0, "o": 0}
        B_wdn = [Buf() for _ in range(NJ)]

        def wdn_task(j):
            stg, Bs_ = ((otmp, [Bot, Bot2]), (xr[0], [Bxr[0]]), (xr[1], [Bxr[1]]))[j % 3]
            P.dma(SP, (lambda e: e.dma_start(out=stg, in_=w_down_dram[j * 128:(j + 1) * 128, :])), writes=Bs_)
            P.op(DVE, (lambda e: e.tensor_copy(out=w_dn_b[:, j * D:(j + 1) * D], in_=stg)), reads=Bs_, writes=[B_wdn[j]])

        pst = {}

        def prep_a(unit, t):
            X1, r0, ydst, y0 = unit
            q = cnt["p"]; cnt["p"] += 1
            hbi, s_, Bs, pb = q % 2, st[q % 4], Bst[q % 4], q % 2
            if t < 4:
                np_ = 128
                xsrc_, Bxs_ = xa, Bxa
                P.dma(SP, (lambda e: e.dma_start(out=xa, in_=X1[r0 + t * 128:r0 + (t + 1) * 128, :])), writes=[Bxa])
            else:
                np_ = 2
                xsrc_, Bxs_ = xr[1], Bxr[1]
                P.dma(SP, (lambda e: e.dma_start(out=xr[1][0:1, :], in_=X1[r0 - 1:r0, :])), writes=[Bxr[1]])
                P.dma(SP, (lambda e: e.dma_start(out=xr[1][1:2, :], in_=X1[r0 + 512:r0 + 513, :])), writes=[Bxr[1]])
            norm_to_bf16(xsrc_[0:np_, :], np_, junkf, s_[:, 0:1], s_[:, 1:2], hb[hbi], Bxs_, Bs, Bhb[hbi], Bjunkf)
            pst[(id(unit), t)] = (hbi, pb, np_)

        def prep_b(unit, t):
            hbi, pb, np_ = pst.pop((id(unit), t))
            def tr(e):
                ins = None
                for k in range(8):
                    ins = e.transpose(out=bkb[pb][:, k * np_:(k + 1) * np_], in_=hb[hbi][0:np_, k * 128:(k + 1) * 128], identity=identb[0:np_, 0:np_])
                return ins
            P.op(PE, tr, reads=[Bhb[hbi], B_con], writes=[BK[pb]])
            if t < 4:
                dst = h2T3[:, :, 1 + t * 128:1 + (t + 1) * 128]
                srcv = bkb[pb].rearrange("p (k n) -> p k n", k=8)
                P.op(ACT, (lambda e: e.activation(out=dst, in_=srcv, func=AF.Copy)), reads=[BK[pb]], writes=[Bh2T[t]])
            else:
                srcv = bkb[pb][:, 0:16].rearrange("p (k n) -> p k n", k=8)
                P.op(ACT, (lambda e: e.activation(out=h2T3[:, :, 0:1], in_=srcv[:, :, 0:1], func=AF.Copy)), reads=[BK[pb]], writes=[Bh2T[t]])
                P.op(ACT, (lambda e: e.activation(out=h2T3[:, :, 513:514], in_=srcv[:, :, 1:2], func=AF.Copy)), reads=[BK[pb]], writes=[Bh2T[t]])

        def w_up_phase(extra=None):
            def bk_of(c):
                return ((0, 1), (2, 3), (4, 5))[c % 3]

            def wu0(c):
                j, c0 = c % NJ, (c // NJ) * 256
                pg, pu = bk_of(c)
                def mmg(e):
                    ins = None
                    for k in range(8):
                        ins = e.matmul(bkf[pg][:, 0:258], lhsT=w_up_b[:, k * 2 * DFF + j * 128:k * 2 * DFF + (j + 1) * 128], rhs=h2T3[:, k, c0:c0 + 258], start=(k == 0), stop=(k == 7))
                    return ins
                def mmup(e):
                    ins = None
                    for k in range(8):
                        ins = e.matmul(bkf[pu][:, 0:258], lhsT=w_up_b[:, k * 2 * DFF + DFF + j * 128:k * 2 * DFF + DFF + (j + 1) * 128], rhs=h2T3[:, k, c0:c0 + 258], start=(k == 0), stop=(k == 7))
                    return ins
                P.op(PE, mmg, reads=Bh2T, writes=[BK[pg]])
                P.op(PE, mmup, reads=Bh2T, writes=[BK[pu]])

            def wu1(c):
                j, cb = c % NJ, c % 2
                pg, pu = bk_of(c)
                jg, ju = j, NJ + j
                cw = lambda t, jj: cwT[:, t * 44 + jj:t * 44 + jj + 1]
                P.op(ACT, (lambda e: e.activation(out=tg[cb], in_=bkf[pg][:, 1:257], func=AF.Identity, scale=cw(1, jg), bias=cbT[:, jg:jg + 1])), reads=[BK[pg], B_con], writes=[Btg[cb]])
                P.op(ACT, (lambda e: e.activation(out=tu[cb], in_=bkf[pu][:, 1:257], func=AF.Identity, scale=cw(1, ju), bias=cbT[:, ju:ju + 1])), reads=[BK[pu], B_con], writes=[Btu[cb]])
                P.op(ACT, (lambda e: e.activation(out=trr[cb], in_=bkf[pu][:, 2:258], func=AF.Copy, scale=cw(2, ju))), reads=[BK[pu], B_con], writes=[Btrr[cb]])
                P.op(DVE, (lambda e: e.scalar_tensor_tensor(out=tg[cb], in0=bkf[pg][:, 0:256], scalar=cw(0, jg), in1=tg[cb], op0=ALU.mult, op1=ALU.add)), reads=[BK[pg], B_con, Btg[cb]], writes=[Btg[cb]])
                P.op(DVE, (lambda e: e.scalar_tensor_tensor(out=tg[cb], in0=bkf[pg][:, 2:258], scalar=cw(2, jg), in1=tg[cb], op0=ALU.mult, op1=ALU.add)), reads=[BK[pg], B_con, Btg[cb]], writes=[Btg[cb]])
                P.op(DVE, (lambda e: e.scalar_tensor_tensor(out=tu[cb], in0=bkf[pu][:, 0:256], scalar=cw(0, ju), in1=tu[cb], op0=ALU.mult, op1=ALU.add)), reads=[BK[pu], B_con, Btu[cb]], writes=[Btu[cb]])

            def wu2(c):
                j, cb, half = c % NJ, c % 2, c // NJ
                c0 = half * 256
                P.op(ACT, (lambda e: e.activation(out=gg[cb], in_=tg[cb], func=AF.Gelu_apprx_tanh)), reads=[Btg[cb]], writes=[Bgg[cb]])
                P.op(POOL, (lambda e: e.tensor_tensor(out=trr[cb], in0=tu[cb], in1=trr[cb], op=ALU.add)), reads=[Btu[cb], Btrr[cb]], writes=[Btrr[cb]])
                P.op(POOL, (lambda e: e.tensor_tensor(out=pT3[:, j, c0:c0 + 256], in0=gg[cb], in1=trr[cb], op=ALU.mult)), reads=[Bgg[cb], Btrr[cb]], writes=[BpT[j][half]])

            n = 2 * NJ
            for it in range(n + 2):
                if extra is not None:
                    extra(it)
                if it < n:
                    wu0(it)
                if 0 <= it - 1 < n:
                    wu1(it - 1)
                if 0 <= it - 2 < n:
                    wu2(it - 2)

        def w_down(unit, t, nh):
            def mmd(e):
                ins = None
                for j in range(NJ):
                    ins = e.matmul(bkf[6 + nh], lhsT=pT3[:, j, t * 128:(t + 1) * 128], rhs=w_dn_b[:, j * D + nh * 512:j * D + (nh + 1) * 512], start=(j == 0), stop=(j == NJ - 1))
                return ins
            P.op(PE, mmd, reads=[BpT[j][t // 2] for j in range(NJ)] + B_wdn, writes=[BK[6 + nh]])

        def post_a(unit, t, nh):
            X1, r0, ydst, y0 = unit
            xb = t % 2
            if nh == 0:
                P.dma(SP, (lambda e: e.dma_start(out=xr[xb], in_=X1[r0 + t * 128:r0 + (t + 1) * 128, :])), writes=[Bxr[xb]])
                P.op(DVE, (lambda e: e.tensor_copy(out=otmp[:, 0:512], in_=bkf[6])), reads=[BK[6]], writes=[Bot])
            else:
                P.op(ACT, (lambda e: e.activation(out=otmp[:, 512:1024], in_=bkf[7], func=AF.Copy)), reads=[BK[7]], writes=[Bot2])

        def post_b(unit, t):
            X1, r0, ydst, y0 = unit
            q = cnt["o"]; cnt["o"] += 1
            s_, Bs = st[q % 4], Bst[q % 4]
            xb = t % 2
            P.op(ACT, (lambda e: e.activation(out=junkf, in_=otmp, func=AF.Square, accum_out=s_[:, 4:5])), reads=[Bot, Bot2], writes=Bjunkf + [Bs])
            rstd_pool(s_[:, 4:5], s_[:, 5:6], 1, 128, 1.0 / D, Bs)
            P.op(DVE, (lambda e: e.scalar_tensor_tensor(out=otmp, in0=otmp, scalar=s_[:, 5:6], in1=gpf_bc, op0=ALU.mult, op1=ALU.mult)), reads=[Bot, Bot2, Bs, B_con], writes=[Bot, Bot2])
            P.op(POOL, (lambda e: e.tensor_tensor(out=xr[xb], in0=otmp, in1=xr[xb], op=ALU.add)), reads=[Bot, Bot2, Bxr[xb]], writes=[Bxr[xb]])
            P.dma(POOL, (lambda e: e.dma_start(out=ydst[y0 + t * 128:y0 + (t + 1) * 128, :], in_=xr[xb])), reads=[Bxr[xb]], writes=[B_Y])

        for t in range(5):
            prep_a(funits[0], t)
            prep_b(funits[0], t)
        for ui, unit in enumerate(funits):
            nxt = funits[ui + 1] if ui + 1 < len(funits) else None
            if ui == 0:
                w_up_phase(extra=(lambda it: wdn_task(it // 2) if (it % 2 == 0 and it // 2 < NJ) else None))
            else:
                w_up_phase()
            if nxt is not None:
                prep_a(nxt, 4)
                prep_a(nxt, 0)
            for t in range(4):
                w_down(unit, t, 0)
                post_a(unit, t, 0)
                if nxt is not None and t == 3:
                    prep_b(nxt, 3)
                w_down(unit, t, 1)
                post_a(unit, t, 1)
                if nxt is not None and t < 3:
                    prep_b(nxt, t)
                    if t == 0:
                        prep_b(nxt, 4)
                    prep_a(nxt, t + 1)
                post_b(unit, t)
        P.barrier()

    funits = []
    for i in range(S_s // 512):
        funits.append((X1_s, 1 + i * 512, ys, i * 512))
    for i in range(OWN // 512):
        funits.append((X1_p, 128 + i * 512, yp, i * 512))
    ffn_pass(funits)
    P.emit()
    return nc


def _bf(a):
    return np.ascontiguousarray(a.astype(np.float32)).astype(ml_dtypes.bfloat16)


def _tables(cfg):
    out = {}
    for name, S, NF in (("s", cfg.S_s, cfg.NF_s), ("p", cfg.S_p, cfg.NF_p)):
        sf = np.arange(NF)[:, None, None]
        ss_ = np.arange(128)[None, :, None]
        kf = np.arange(128)[None, None, :]
        s = (NF * ss_ + sf).astype(np.int64)
        ph = ((kf * s) % S).astype(np.float64) * (2 * np.pi / S)
        tab = np.stack([np.cos(ph)[:, :, 0:KH], -np.sin(ph)[:, :, 0:KH]], axis=2)
        out["tabA_" + name] = _bf(np.ascontiguousarray(tab.transpose(1, 0, 2, 3)).reshape(128, NF * 2 * KH))
        c = np.arange(128)[:, None]
        m = np.arange(128)[None, :]
        ph2 = ((c * m) % 128).astype(np.float64) * (2 * np.pi / 128)
        sc = 1.0 / np.sqrt(S * 128.0)
        out["cs_" + name] = _bf(np.stack([np.cos(ph2) * sc, np.sin(ph2) * sc], axis=1))
    return out


def _tabC(NF, ks_vals, mirror=False):
    sf = np.arange(NF)[:, None]
    ks = np.asarray(ks_vals, dtype=np.int64)[None, :] + (1 if mirror else 0)
    ph = ((sf * ks) % NF).astype(np.float64) * (2 * np.pi / NF)
    c, s = np.cos(ph), np.sin(ph)
    if not mirror:
        t0 = np.concatenate([c, -s], axis=1)
        t1 = np.concatenate([s, c], axis=1)
    else:
        t0 = np.concatenate([c, -s], axis=1)
        t1 = np.concatenate([-s, -c], axis=1)
    return _bf(np.stack([t0, t1], axis=1))


_CACHE = {}


def kernel_impl(cfg, x_prompt, x_sample, pre_mix_norm, w_in, sgu_norm, w_spatial, b_spatial, w_out,
                post_mix_norm, pre_ffn_norm, w_up, conv_w, conv_b, w_down, post_ffn_norm):
    f = lambda a: np.ascontiguousarray(np.asarray(a, dtype=np.float32))
    x_prompt, x_sample = f(x_prompt), f(x_sample)
    S_s, S_p, OWN, EXT = cfg.S_s, cfg.S_p, cfg.OWN, cfg.EXT
    key = (S_s, S_p, cfg.debug)
    if key not in _CACHE:
        _CACHE[key] = build_program(cfg)
    nc = _CACHE[key]
    tabs = _tables(cfg)
    common = dict(
        xpf=np.ascontiguousarray(x_prompt[0].reshape(128, cfg.NF_p, D).transpose(1, 0, 2)).reshape(S_p, D),
        g_pre=np.ascontiguousarray(np.stack([f(pre_mix_norm)[0].reshape(8, 128), f(pre_ffn_norm)[0].reshape(8, 128)])),
        w_in=f(w_in)[0], sgain=f(sgu_norm)[0].reshape(512), w_sp=f(w_spatial)[0], b_sp=f(b_spatial)[0].reshape(512),
        w_out=f(w_out)[0], g_pm=f(post_mix_norm)[0], w_up=f(w_up)[0],
        conv_w=np.ascontiguousarray(f(conv_w)[0].reshape(3 * 44, 128)), conv_b=np.ascontiguousarray(f(conv_b)[0].reshape(44, 128)),
        w_down=f(w_down)[0], g_pf=f(post_ffn_norm)[0],
        identb=_bf(np.eye(128)), identf=np.eye(128, dtype=np.float32),
        tabA_s=tabs["tabA_s"], tabA_p=tabs["tabA_p"], cs_s=tabs["cs_s"], cs_p=tabs["cs_p"],
        tabC_s=_tabC(cfg.NF_s, np.arange(cfg.NF_s)), tabM_s=_tabC(cfg.NF_s, np.arange(cfg.NF_s), mirror=True),
    )
    in_maps = []
    for j in range(NCORES):
        xpo = np.zeros((EXT, D), np.float32)
        lo, hi = j * OWN - 128, (j + 1) * OWN + 128
        a, b = max(lo, 0), min(hi, S_p)
        xpo[a - lo:b - lo] = x_prompt[0, a:b]
        mask = np.zeros((128, 2), np.float32)
        mask[:, 0] = 1.0 if j > 0 else 0.0
        mask[:, 1] = 1.0 if j < NCORES - 1 else 0.0
        ks0 = j * (OWN // 128) - 1
        m = dict(common)
        m.update(xs=x_sample[j], xst=np.ascontiguousarray(x_sample[j].reshape(128, cfg.NF_s, D).transpose(1, 0, 2)).reshape(S_s, D), xpo=xpo, mask=mask, tabC_p=_tabC(cfg.NF_p, np.arange(ks0, ks0 + cfg.NKS_p)),
                 tabM_p=_tabC(cfg.NF_p, np.arange(ks0, ks0 + cfg.NKS_p), mirror=True))
        in_maps.append(m)
    res = run_bass_kernel_spmd(nc, in_maps, core_ids=list(range(NCORES)))
    y_prompt = np.concatenate([r["yp"] for r in res.results], axis=0)[None]
    y_sample = np.stack([r["ys"] for r in res.results], axis=0)
    if cfg.debug:
        return (y_prompt.astype(np.float32), y_sample.astype(np.float32)), res.results
    return (y_prompt.astype(np.float32), y_sample.astype(np.float32))


def kernel(**inputs):
    return kernel_impl(Cfg(), **inputs)
```

```python
from contextlib import ExitStack
import numpy as np
import ml_dtypes
import concourse.bass as bass
import concourse.mybir as mybir
from concourse.bass_utils import run_bass_kernel_spmd

F32 = mybir.dt.float32
BF16 = mybir.dt.bfloat16
AF = mybir.ActivationFunctionType
ALU = mybir.AluOpType

D = 1024
FW = 512
INC = 1536
DFF = 2816
NJ = DFF // 128
EPS = 1e-6
NCORES = 8
KH = 65

PE, ACT, DVE, POOL, SP = "tensor", "scalar", "vector", "gpsimd", "sync"
ENGS = (PE, ACT, DVE, POOL, SP)
DMA_SLOTS = 8
SAME_ENGINE_SYNC = True


class Buf:
    __slots__ = ("name", "w", "r", "excl")

    def __init__(self, name="", excl=False):
        self.name = name
        self.w = None
        self.r = {}
        self.excl = excl


class Prog:
    def __init__(self, nc):
        self.nc = nc
        self.ops = {e: [] for e in ENGS}
        self.cnt = {e: 0 for e in ENGS}
        self.known = {e: {} for e in ENGS}
        self.dma_n = {e: 0 for e in ENGS}
        self.semkeys = set()

    def _deps(self, eng, reads, writes, extra=(), own=None):
        need = {}

        def add(t):
            if t is None:
                return
            k, v = t
            if need.get(k, 0) < v:
                need[k] = v

        for b in reads:
            add(b.w)
            if b.excl:
                for k, v in b.r.items():
                    if k != own:
                        add((k, v))
        for b in writes:
            add(b.w)
            for k, v in b.r.items():
                add((k, v))
        for t in extra:
            add(t)
        out = []
        kn = self.known[eng]
        for k, v in need.items():
            if (not SAME_ENGINE_SYNC) and k == ("c", eng):
                continue
            if kn.get(k, 0) >= v:
                continue
            kn[k] = v
            out.append((k, v))
        return out

    def _mark(self, tok, reads, writes):
        k, v = tok
        for b in reads:
            if b.r.get(k, 0) < v:
                b.r[k] = v
        for b in writes:
            b.w = tok
            b.r = {}

    def op(self, eng, fn, reads=(), writes=()):
        waits = self._deps(eng, reads, writes, own=("c", eng))
        self.cnt[eng] += 1
        key = ("c", eng)
        self.semkeys.add(key)
        tok = (key, self.cnt[eng])
        self.ops[eng].append((waits, fn, key, 1))
        self._mark(tok, reads, writes)
        return tok

    def dma(self, q, fn, reads=(), writes=()):
        n = self.dma_n[q]
        self.dma_n[q] += 1
        slot = n % DMA_SLOTS
        key = ("d", q, slot)
        self.semkeys.add(key)
        val = 16 * (n // DMA_SLOTS + 1)
        extra = [(key, val - 16)] if val > 16 else []
        waits = self._deps(q, reads, writes, extra)
        tok = (key, val)
        self.ops[q].append((waits, fn, key, 16))
        self._mark(tok, reads, writes)
        return tok

    def barrier(self, engs=ENGS):
        allt = []
        for e in ENGS:
            if self.cnt[e]:
                allt.append((("c", e), self.cnt[e]))
            n = self.dma_n[e]
            for s in range(min(n, DMA_SLOTS)):
                cntv = (n - 1 - s) // DMA_SLOTS + 1
                allt.append((("d", e, s), 16 * cntv))
        for e in engs:
            waits = self._deps(e, (), (), allt)
            if waits:
                self.ops[e].append((waits, None, None, 0))

    def emit(self):
        nc = self.nc
        with ExitStack() as es:
            sems = {}
            for k in sorted(self.semkeys, key=str):
                sems[k] = es.enter_context(nc.semaphore("s_" + "_".join(str(x) for x in k)))
            block = es.enter_context(nc.Block())

            def make(eng):
                def body(e):
                    for waits, fn, key, inc in self.ops[eng]:
                        for k, v in waits:
                            e.wait_ge(sems[k], v)
                        if fn is not None:
                            fn(e).then_inc(sems[key], inc)
                return body

            for eng in ENGS:
                if self.ops[eng]:
                    getattr(block, eng)(make(eng))


class Arena:
    def __init__(self, nc, name, nbytes):
        self.t = nc.alloc_sbuf_tensor(name, [128, nbytes // 2], BF16)
        self.ap = self.t[:]
        self.off = 0
        self.cap = nbytes

    def mark(self):
        return self.off

    def reset(self, m):
        self.off = m

    def alloc(self, n_elems, dtype=BF16):
        esz = 4 if dtype == F32 else 2
        nb = (n_elems * esz + 63) // 64 * 64
        assert self.off + nb <= self.cap, f"arena overflow {self.off}+{nb}>{self.cap}"
        a = self.ap[:, self.off // 2:(self.off + nb) // 2]
        self.off += nb
        if dtype == F32:
            a = a.bitcast(F32)
        return a[:, 0:n_elems]


class Cfg:
    def __init__(self, S_s=8192, S_p=16384, debug=False):
        self.S_s = S_s
        self.S_p = S_p
        self.OWN = S_p // NCORES
        self.EXT = self.OWN + 256
        self.NF_s = S_s // 128
        self.NF_p = S_p // 128
        self.NKS_p = self.OWN // 128 + 2
        self.debug = debug
        assert S_s % 512 == 0 and self.OWN % 512 == 0


def build_program(cfg):
    nc = bass.Bass("TRN2", target_bir_lowering=False)
    P = Prog(nc)
    dbg = cfg.debug

    def din(name, shape, dt=F32):
        return nc.dram_tensor(name, list(shape), dt, kind="ExternalInput").ap()

    def dout(name, shape, dt=F32):
        return nc.dram_tensor(name, list(shape), dt, kind="ExternalOutput").ap()

    def dscr(name, shape, dt):
        kind = "ExternalOutput" if dbg else "Internal"
        return nc.dram_tensor(name, list(shape), dt, kind=kind).ap()

    S_s, S_p, OWN, EXT = cfg.S_s, cfg.S_p, cfg.OWN, cfg.EXT
    xs = din("xs", [S_s, D])
    xpf = din("xpf", [S_p, D])
    xst = din("xst", [S_s, D])
    xpo = din("xpo", [EXT, D])
    maskd = din("mask", [128, 2])
    g_pre = din("g_pre", [2, 8, 128])
    w_in = din("w_in", [D, INC])
    sgain = din("sgain", [512])
    w_sp = din("w_sp", [4, 128, 128])
    b_sp = din("b_sp", [512])
    w_out = din("w_out", [D, D])
    g_pm = din("g_pm", [D])
    w_up = din("w_up", [D, 2 * DFF])
    conv_w = din("conv_w", [132, 128])
    conv_b = din("conv_b", [44, 128])
    w_down = din("w_down", [DFF, D])
    g_pf = din("g_pf", [D])
    identb_d = din("identb", [128, 128], BF16)
    identf_d = din("identf", [128, 128])
    tabA_s = din("tabA_s", [128, cfg.NF_s * 2 * KH], BF16)
    tabA_p = din("tabA_p", [128, cfg.NF_p * 2 * KH], BF16)
    tabC_s = din("tabC_s", [cfg.NF_s, 2, 2 * cfg.NF_s], BF16)
    tabC_p = din("tabC_p", [cfg.NF_p, 2, 2 * cfg.NKS_p], BF16)
    tabM_s = din("tabM_s", [cfg.NF_s, 2, 2 * cfg.NF_s], BF16)
    tabM_p = din("tabM_p", [cfg.NF_p, 2, 2 * cfg.NKS_p], BF16)
    cs_s = din("cs_s", [128, 2, 128], BF16)
    cs_p = din("cs_p", [128, 2, 128], BF16)
    ys = dout("ys", [S_s, D])
    yp = dout("yp", [OWN, D])
    A_s = dscr("A_s", [KH, cfg.NF_s, 2 * FW], BF16)
    A_p = dscr("A_p", [KH, cfg.NF_p, 2 * FW], BF16)
    FT_s = dscr("FT_s", [4, 128, S_s], BF16)
    FT_p = dscr("FT_p", [4, 128, EXT], BF16)
    X1_s = dscr("X1_s", [S_s + 2, D], F32)
    X1_p = dscr("X1_p", [EXT, D], F32)

    banks = [nc.alloc_psum_tensor(f"bank{i}", [128, 512], F32) for i in range(8)]
    bkf = [b[:] for b in banks]
    bkb = [b[:].bitcast(BF16) for b in banks]
    BK = [Buf(f"bank{i}", excl=True) for i in range(8)]

    CON = Arena(nc, "con", 16896)
    REG = Arena(nc, "reg", 132 * 1024)
    WRK = Arena(nc, "wrk", 58 * 1024)

    identb = CON.alloc(128); identf = CON.alloc(128, F32)
    gcol = CON.alloc(16, F32)
    cwT = CON.alloc(132, F32)
    cbT = CON.alloc(44, F32)
    maskt = CON.alloc(2, F32)
    epst = CON.alloc(1, F32)
    mhalf = CON.alloc(4, F32)
    bcol = CON.alloc(4, F32)
    gpm_bc = CON.alloc(D, F32); gpf_bc = CON.alloc(D, F32)
    sgain_bc = CON.alloc(512, F32); bsp_bc = CON.alloc(512, F32)
    WsT = CON.alloc(512)
    cs128 = [CON.alloc(256), CON.alloc(256)]
    B_con = Buf("con")

    def ld(dst, src, q=SP, bufs=(B_con,)):
        P.dma(q, lambda e: e.dma_start(out=dst, in_=src), writes=list(bufs))

    ld(identb, identb_d[:, :]); ld(identf, identf_d[:, :]); ld(maskt, maskd[:, :])
    ld(gpm_bc, g_pm.partition_broadcast(128)); ld(gpf_bc, g_pf.partition_broadcast(128))
    ld(sgain_bc, sgain.partition_broadcast(128)); ld(bsp_bc, b_sp.partition_broadcast(128))
    ld(cs128[0], cs_s.rearrange("c t m -> c (t m)")); ld(cs128[1], cs_p.rearrange("c t m -> c (t m)"))
    P.op(DVE, lambda e: e.memset(epst, EPS), writes=[B_con])
    P.op(DVE, lambda e: e.memset(mhalf, -0.5), writes=[B_con])
    m0 = WRK.mark()
    zrow = WRK.alloc(D, F32)
    P.op(DVE, lambda e: e.memset(zrow, 0.0), writes=[B_con])
    B_x1s = Buf("x1s_dram")
    P.dma(SP, lambda e: e.dma_start(out=X1_s[0:1, :], in_=zrow[0:1, :]), reads=[B_con], writes=[B_x1s])
    P.dma(SP, lambda e: e.dma_start(out=X1_s[S_s + 1:S_s + 2, :], in_=zrow[0:1, :]), reads=[B_con], writes=[B_x1s])

    st_rows = WRK.alloc(128, F32); st_rows2 = WRK.alloc(128, F32); st_rows3 = WRK.alloc(128, F32); st_rows4 = WRK.alloc(128, F32); st_rows5 = WRK.alloc(128, F32)
    B_st = Buf("st")
    ld(st_rows[0:16, :], g_pre.rearrange("a k p -> (a k) p"), bufs=(B_st,))
    ld(st_rows2[0:128, :], conv_w[0:128, :], bufs=(B_st,))
    ld(st_rows3[0:4, :], conv_w[128:132, :], bufs=(B_st,))
    ld(st_rows4[0:44, :], conv_b[:, :], bufs=(B_st,))
    ld(st_rows5[0:4, :], b_sp.rearrange("(h p) -> h p", h=4), bufs=(B_st,))

    def trf(e):
        e.transpose(out=bkf[0][:, 0:16], in_=st_rows[0:16, :], identity=identf[0:16, 0:16])
        e.transpose(out=bkf[0][:, 16:144], in_=st_rows2[0:128, :], identity=identf[:, :])
        e.transpose(out=bkf[0][:, 144:148], in_=st_rows3[0:4, :], identity=identf[0:4, 0:4])
        e.transpose(out=bkf[0][:, 148:192], in_=st_rows4[0:44, :], identity=identf[0:44, 0:44])
        return e.transpose(out=bkf[0][:, 192:196], in_=st_rows5[0:4, :], identity=identf[0:4, 0:4])
    P.op(PE, trf, reads=[B_st, B_con], writes=[BK[0]])
    P.op(DVE, lambda e: e.tensor_copy(out=gcol, in_=bkf[0][:, 0:16]), reads=[BK[0]], writes=[B_con])
    P.op(DVE, lambda e: e.tensor_copy(out=cwT, in_=bkf[0][:, 16:148]), reads=[BK[0]], writes=[B_con])
    P.op(DVE, lambda e: e.tensor_copy(out=cbT, in_=bkf[0][:, 148:192]), reads=[BK[0]], writes=[B_con])
    P.op(DVE, lambda e: e.tensor_copy(out=bcol, in_=bkf[0][:, 192:196]), reads=[BK[0]], writes=[B_con])
    wsp_f = WRK.alloc(512, F32); wsp_b = WRK.alloc(512)
    ld(wsp_f, w_sp.rearrange("h p q -> p h q"), bufs=(B_st,))
    P.op(DVE, lambda e: e.tensor_copy(out=wsp_b, in_=wsp_f), reads=[B_st], writes=[B_st])

    def trw(e):
        ins = None
        for hd in range(4):
            ins = e.transpose(out=bkb[1][:, hd * 128:(hd + 1) * 128], in_=wsp_b[:, hd * 128:(hd + 1) * 128], identity=identb)
        return ins
    P.op(PE, trw, reads=[B_st, B_con], writes=[BK[1]])
    P.op(DVE, lambda e: e.tensor_copy(out=WsT, in_=bkb[1][:, 0:512]), reads=[BK[1]], writes=[B_con])
    P.barrier()
    WRK.reset(m0)

    def load_weight(dst3, src, kchunks, ncols, scale_col0=None, stage_cols=1536):
        mk = WRK.mark()
        NSTG = 6
        stg = [WRK.alloc(stage_cols, F32) for _ in range(NSTG)]
        B_stg = [Buf() for _ in range(NSTG)]
        B_w = Buf("w")
        i = 0
        for k in range(kchunks):
            for c0 in range(0, ncols, stage_cols):
                c1 = min(ncols, c0 + stage_cols)
                s = stg[i % NSTG]; bs = B_stg[i % NSTG]
                P.dma(SP, (lambda e, s=s, k=k, c0=c0, c1=c1: e.dma_start(out=s[:, 0:c1 - c0], in_=src[k * 128:(k + 1) * 128, c0:c1])), writes=[bs])
                dsl = dst3[:, k * ncols + c0:k * ncols + c1]
                eng = DVE if i % 2 == 0 else ACT
                if scale_col0 is None:
                    if eng == DVE:
                        P.op(DVE, (lambda e, s=s, dsl=dsl, n=c1 - c0: e.tensor_copy(out=dsl, in_=s[:, 0:n])), reads=[bs], writes=[B_w])
                    else:
                        P.op(ACT, (lambda e, s=s, dsl=dsl, n=c1 - c0: e.activation(out=dsl, in_=s[:, 0:n], func=AF.Copy)), reads=[bs], writes=[B_w])
                else:
                    sc = gcol[:, scale_col0 + k:scale_col0 + k + 1]
                    if eng == DVE:
                        P.op(DVE, (lambda e, s=s, dsl=dsl, n=c1 - c0, sc=sc: e.tensor_scalar(out=dsl, in0=s[:, 0:n], scalar1=sc, scalar2=None, op0=ALU.mult)), reads=[bs, B_con], writes=[B_w])
                    else:
                        P.op(ACT, (lambda e, s=s, dsl=dsl, n=c1 - c0, sc=sc: e.activation(out=dsl, in_=s[:, 0:n], func=AF.Copy, scale=sc)), reads=[bs, B_con], writes=[B_w])
                i += 1
        P.barrier()
        WRK.reset(mk)

    w_in_b = REG.alloc(8 * INC)
    w_out_b = REG.alloc(8 * D)
    load_weight(w_in_b, w_in, 8, INC, scale_col0=0)
    load_weight(w_out_b, w_out, 8, D)
    regA_mark = REG.mark()

    def w_in_sl(k, c0, c1):
        return w_in_b[:, k * INC + c0:k * INC + c1]

    def w_out_sl(k, c0, c1):
        return w_out_b[:, k * D + c0:k * D + c1]

    def rstd_pool(ss, rstd, n, np_, inv_n, Bss):
        P.op(POOL, lambda e: e.tensor_scalar(out=rstd[0:np_, 0:n], in0=ss[0:np_, 0:n], scalar1=inv_n, scalar2=EPS, op0=ALU.mult, op1=ALU.add), reads=[Bss], writes=[Bss])
        P.op(POOL, lambda e: e.tensor_tensor(out=rstd[0:np_, 0:n], in0=rstd[0:np_, 0:n], in1=mhalf[0:np_, 0:n], op=ALU.pow), reads=[Bss, B_con], writes=[Bss])

    def norm_to_bf16(x_ap, np_, junk, ss, rstd, hb, Bx, Bss, Bh, Bjunk, scale_eng=DVE):
        P.op(ACT, lambda e: e.activation(out=junk[0:np_, :], in_=x_ap, func=AF.Square, accum_out=ss[0:np_, :]), reads=[Bx], writes=(list(Bjunk) if isinstance(Bjunk, (list, tuple)) else [Bjunk]) + [Bss])
        rstd_pool(ss, rstd, 1, np_, 1.0 / D, Bss)
        if scale_eng == DVE:
            P.op(DVE, lambda e: e.tensor_scalar(out=hb[0:np_, :], in0=x_ap, scalar1=rstd[0:np_, :], scalar2=None, op0=ALU.mult), reads=[Bx, Bss], writes=[Bh])
        else:
            P.op(ACT, lambda e: e.activation(out=hb[0:np_, :], in_=x_ap, func=AF.Copy, scale=rstd[0:np_, :]), reads=[Bx, Bss], writes=[Bh])

    def run_pipeline(n, stages, older_first=True):
        ns = len(stages)
        for it in range(n + ns - 1):
            for k in (reversed(range(ns)) if older_first else range(ns)):
                i = it - k
                if 0 <= i < n:
                    stages[k](i)

    def phase1(xsrc, S, NF, tabA, Ascr):
        mk_r, mk_w = REG.mark(), WRK.mark()
        NX = 6
        xt = [REG.alloc(D, F32) for _ in range(NX)]
        junk = REG.alloc(D)
        hb = [REG.alloc(D) for _ in range(2)]
        hT = [REG.alloc(D) for _ in range(2)]
        zt = [REG.alloc(FW) for _ in range(2)]
        tball = REG.alloc(NF * 2 * KH)
        B_tb = Buf()
        P.dma(SP, lambda e: e.dma_start(out=tball, in_=tabA[:, :]), writes=[B_tb])
        NA = 4
        At = [REG.alloc(2 * FW) for _ in range(NA)]
        st = [WRK.alloc(2, F32) for _ in range(4)]
        Bxt = [Buf() for _ in range(NX)]; Bjunk = Buf(); Bhb = [Buf(), Buf()]
        BhT = [Buf(), Buf()]; Bzt = [Buf(), Buf()]; Btb = None
        BAt = [Buf() for _ in range(NA)]; Bst = [Buf() for _ in range(4)]
        B_A = Buf("Ascr")
        xv = xsrc.rearrange("(f i) d -> f i d", i=128)

        def s0(i):
            P.dma(SP, (lambda e: e.dma_start(out=xt[i % NX], in_=xv[i])), writes=[Bxt[i % NX]])

        def s1(i):
            s_ = st[i % 4]
            P.op(ACT, (lambda e: e.activation(out=junk, in_=xt[i % NX], func=AF.Square, accum_out=s_[:, 0:1])), reads=[Bxt[i % NX]], writes=[Bjunk, Bst[i % 4]])

        def s1b(i):
            s_ = st[i % 4]
            rstd_pool(s_[:, 0:1], s_[:, 1:2], 1, 128, 1.0 / D, Bst[i % 4])

        def s1c(i):
            s_ = st[i % 4]
            P.op(DVE, (lambda e: e.tensor_scalar(out=hb[i % 2], in0=xt[i % NX], scalar1=s_[:, 1:2], scalar2=None, op0=ALU.mult)), reads=[Bxt[i % NX], Bst[i % 4]], writes=[Bhb[i % 2]])

        def s2(i):
            b, pb = i % 2, i % 2
            def tr(e):
                ins = None
                for k in range(8):
                    ins = e.transpose(out=bkb[pb][:, k * 128:(k + 1) * 128], in_=hb[b][:, k * 128:(k + 1) * 128], identity=identb)
                return ins
            P.op(PE, tr, reads=[Bhb[b], B_con], writes=[BK[pb]])

        def s3(i):
            b, pb = i % 2, i % 2
            P.op(ACT, (lambda e: e.activation(out=hT[b], in_=bkb[pb], func=AF.Copy)), reads=[BK[pb]], writes=[BhT[b]])

        def s4(i):
            b, zb = i % 2, 2 + i % 2
            def mmz(e):
                ins = None
                for k in range(8):
                    ins = e.matmul(bkf[zb], lhsT=hT[b][:, k * 128:(k + 1) * 128], rhs=w_in_sl(k, 0, FW), start=(k == 0), stop=(k == 7))
                return ins
            P.op(PE, mmz, reads=[BhT[b]], writes=[BK[zb]])

        def s5(i):
            b, zb = i % 2, 2 + i % 2
            P.op(DVE, (lambda e: e.tensor_copy(out=zt[b], in_=bkf[zb])), reads=[BK[zb]], writes=[Bzt[b]])

        def s6(i):
            b, ab = i % 2, 4 + 2 * (i % 2)
            tc0 = i * 2 * KH
            P.op(PE, (lambda e: e.matmul(bkf[ab][0:KH, :], lhsT=tball[:, tc0:tc0 + KH], rhs=zt[b], start=True, stop=True)), reads=[B_tb, Bzt[b]], writes=[BK[ab]])
            P.op(PE, (lambda e: e.matmul(bkf[ab + 1][0:KH, :], lhsT=tball[:, tc0 + KH:tc0 + 2 * KH], rhs=zt[b], start=True, stop=True)), reads=[B_tb, Bzt[b]], writes=[BK[ab + 1]])

        def s7(i):
            b, ab = i % NA, 4 + 2 * (i % 2)
            At4 = At[b].rearrange("p (g t c) -> p g t c", g=4, t=2)
            P.op(ACT, (lambda e: e.activation(out=At4[0:KH, :, 0, :], in_=bkf[ab][0:KH, :].rearrange("p (g c) -> p g c", g=4), func=AF.Copy)), reads=[BK[ab]], writes=[BAt[b]])
            P.op(DVE, (lambda e: e.tensor_copy(out=At4[0:KH, :, 1, :], in_=bkf[ab + 1][0:KH, :].rearrange("p (g c) -> p g c", g=4))), reads=[BK[ab + 1]], writes=[BAt[b]])
            P.dma(POOL, (lambda e: e.dma_start(out=Ascr[:, i, :], in_=At[b][0:KH, :])), reads=[BAt[b]], writes=[B_A])

        run_pipeline(NF, [s0, s1, s1b, s1c, s2, s3, s4, s5, s6, s7], older_first=False)
        P.barrier()
        REG.reset(mk_r); WRK.reset(mk_w)

    def phase1b(NF, NKS, ntok, tabC, tabM, cs, Ascr, FTscr, NG):
        mk_r, mk_w = REG.mark(), WRK.mark()
        tC = REG.alloc(4 * NKS); tM = REG.alloc(4 * NKS)
        B_tC = Buf()
        P.dma(SP, lambda e: e.dma_start(out=tC[0:NF, :], in_=tabC.rearrange("s t n -> s (t n)")), writes=[B_tC])
        P.dma(SP, lambda e: e.dma_start(out=tM[0:NF, :], in_=tabM.rearrange("s t n -> s (t n)")), writes=[B_tC])
        XT = [REG.alloc(2 * ntok) for _ in range(NG)]
        B_XT = [Buf() for _ in range(NG)]
        NB = 3
        XW = NG * 256
        Ain = [REG.alloc(4 * XW) for _ in range(NB)]
        BAin = [Buf() for _ in range(NB)]
        FTo = [REG.alloc(512) for _ in range(2)]
        BFTo = [Buf() for _ in range(2)]
        B_FT = Buf("FTscr")
        Av = Ascr.rearrange("k s (gp x) -> k s gp x", gp=4 // NG)
        cpg = 8 * NKS
        gpb = max(1, min(NG, 512 // cpg))
        nbk = (NG + gpb - 1) // gpb
        assert nbk <= 2
        it = 0
        ld = 0
        ob = 0
        for gp in range(4 // NG):
            for kf0 in list(range(0, 64, 4)) + [64]:
                nk = 4 if kf0 < 64 else 1
                b = ld % NB
                ld += 1
                src = Av[kf0:kf0 + nk, :, gp, :].rearrange("k s x -> s k x")
                P.dma(SP, (lambda e, b=b, src=src, nk=nk: e.dma_start(out=Ain[b][0:NF, 0:nk * XW].rearrange("s (k x) -> s k x", k=nk), in_=src)), writes=[BAin[b]])
                for mirror in ((False, True) if kf0 < 64 else (False,)):
                    par = it % 2
                    it += 1
                    kls = [kl for kl in range(nk) if not (mirror and kf0 + kl == 0)]
                    tab = tM if mirror else tC
                    for bi in range(nbk):
                        pb = 2 * par + bi
                        gls = list(range(bi * gpb, min(NG, (bi + 1) * gpb)))
                        def mmc(e, b=b, pb=pb, gls=gls, kls=kls, tab=tab, mirror=mirror):
                            ins = None
                            for gi, gl in enumerate(gls):
                                for kl in kls:
                                    slot = (3 - kl) if mirror else kl
                                    o = bkf[pb][:, gi * cpg + slot * 2 * NKS:gi * cpg + (slot + 1) * 2 * NKS]
                                    base = kl * XW + gl * 256
                                    e.matmul(o, lhsT=Ain[b][0:NF, base:base + 128], rhs=tab[0:NF, 0:2 * NKS], start=True, stop=False)
                                    ins = e.matmul(o, lhsT=Ain[b][0:NF, base + 128:base + 256], rhs=tab[0:NF, 2 * NKS:4 * NKS], start=False, stop=True)
                            return ins
                        P.op(PE, mmc, reads=[BAin[b], B_tC], writes=[BK[pb]])
                        slots = sorted(((3 - kl) if mirror else kl) for kl in kls)
                        s0_, ns = slots[0], len(slots)
                        kdst0 = (128 - kf0 - 3 + s0_) if mirror else (kf0 + s0_)
                        for gi, gl in enumerate(gls):
                            src_v = bkf[pb][:, gi * cpg + s0_ * 2 * NKS:gi * cpg + (s0_ + ns) * 2 * NKS].rearrange("p (f r s) -> p f r s", f=ns, r=2)
                            dst_v = XT[gl].rearrange("p (r f s) -> p f r s", r=2, f=128, s=NKS)[:, kdst0:kdst0 + ns, :, :]
                            if bi == 0:
                                P.op(DVE, (lambda e, s_=src_v, d=dst_v: e.tensor_copy(out=d, in_=s_)), reads=[BK[pb]], writes=[B_XT[gl]])
                            else:
                                P.op(ACT, (lambda e, s_=src_v, d=dst_v: e.activation(out=d, in_=s_, func=AF.Copy)), reads=[BK[pb]], writes=[B_XT[gl]])
            for gl in range(NG):
                g = NG * gp + gl
                XTv = XT[gl].rearrange("p (r f s) -> p r s f", r=2, f=128, s=NKS)
                for c0 in range(0, ntok, 512):
                    c1 = min(ntok, c0 + 512)
                    n = c1 - c0
                    pb = 4 + ob % 2
                    fb = ob % 2
                    ob += 1
                    def mmch(e, c0=c0, n=n, pb=pb, XTv=XTv):
                        k0, nk_ = c0 // 128, n // 128
                        e.matmul(bkf[pb][:, 0:n], lhsT=cs[:, 0:128], rhs=XTv[:, 0, k0:k0 + nk_, :], start=True, stop=False)
                        return e.matmul(bkf[pb][:, 0:n], lhsT=cs[:, 128:256], rhs=XTv[:, 1, k0:k0 + nk_, :], start=False, stop=True)
                    P.op(PE, mmch, reads=[B_XT[gl], B_con], writes=[BK[pb]])
                    P.op(DVE, (lambda e, n=n, pb=pb, fb=fb: e.tensor_copy(out=FTo[fb][:, 0:n], in_=bkf[pb][:, 0:n])), reads=[BK[pb]], writes=[BFTo[fb]])
                    P.dma(POOL, (lambda e, g=g, c0=c0, n=n, fb=fb: e.dma_start(out=FTscr[g, :, c0:c0 + n], in_=FTo[fb][:, 0:n])), reads=[BFTo[fb]], writes=[B_FT])
        P.barrier()
        REG.reset(mk_r); WRK.reset(mk_w)

    phase1(xst, S_s, cfg.NF_s, tabA_s, A_s)
    phase1(xpf, S_p, cfg.NF_p, tabA_p, A_p)
    phase1b(cfg.NF_s, cfg.NF_s, S_s, tabC_s, tabM_s, cs128[0], A_s, FT_s, 2)
    phase1b(cfg.NF_p, cfg.NKS_p, EXT, tabC_p, tabM_p, cs128[1], A_p, FT_p, 4)

    def mixer_pass(units):
        mk_r, mk_w = REG.mark(), WRK.mark()
        tiles = []
        for (xsrc, r0, FTscr, X1, x1r0, nt, mcol) in units:
            for t in range(nt):
                tiles.append((xsrc, r0 + t * 128, FTscr, X1, x1r0 + t * 128, mcol))
        NT = len(tiles)
        def ring(n, elems, dt=BF16):
            return [REG.alloc(elems, dt) for _ in range(n)], [Buf() for _ in range(n)]
        xn, Bxn = ring(3, D, F32)
        xr, Bxr = ring(3, D, F32)
        junk = REG.alloc(D); Bjunk = Buf()
        junk2 = REG.alloc(D, F32); Bjunk2 = Buf()
        Btmp2 = [Buf() for _ in range(3)]
        hb, Bhb = ring(2, D)
        hT, BhT = ring(2, D)
        uf, Buf_ = ring(5, 512)
        vf, Bvf = ring(3, 512, F32)
        sq, Bsq = ring(2, 512, F32)
        vn, Bvn = ring(2, 512)
        gt, Bgt = ring(2, 512)
        gT, BgT = ring(2, 512)
        ft, Bft = ring(3, 512)
        tmp, Btmp = ring(3, D, F32)
        NS = 20
        st, Bst = [WRK.alloc(16, F32) for _ in range(NS)], [Buf() for _ in range(NS)]
        B_X1 = Buf("X1scr")
        bA = (0, 7)
        bU, bV, bS, bG, bO = 1, 2, 3, 4, (5, 6)

        def s_load(i):
            xsrc, r0, FTscr, X1, x1r0, mcol = tiles[i]
            P.dma(SP, (lambda e: e.dma_start(out=xn[i % 3], in_=xsrc[r0:r0 + 128, :])), writes=[Bxn[i % 3]])

        def s_sq(i):
            s_ = st[i % NS]
            P.op(ACT, (lambda e: e.activation(out=junk, in_=xn[i % 3], func=AF.Square, accum_out=s_[:, 0:1])), reads=[Bxn[i % 3]], writes=[Bjunk, Bst[i % NS]])
            rstd_pool(s_[:, 0:1], s_[:, 1:2], 1, 128, 1.0 / D, Bst[i % NS])

        def s_scale(i):
            s_ = st[i % NS]
            P.op(ACT, (lambda e: e.activation(out=hb[i % 2], in_=xn[i % 3], func=AF.Copy, scale=s_[:, 1:2])), reads=[Bxn[i % 3], Bst[i % NS]], writes=[Bhb[i % 2]])

        def s_tr(i):
            pb = bA[i % 2]
            def tr(e):
                ins = None
                for k in range(8):
                    ins = e.transpose(out=bkb[pb][:, k * 128:(k + 1) * 128], in_=hb[i % 2][:, k * 128:(k + 1) * 128], identity=identb)
                return ins
            P.op(PE, tr, reads=[Bhb[i % 2], B_con], writes=[BK[pb]])

        def s_evh(i):
            pb = bA[i % 2]
            P.op(ACT, (lambda e: e.activation(out=hT[i % 2], in_=bkb[pb], func=AF.Copy)), reads=[BK[pb]], writes=[BhT[i % 2]])

        def s_uv(i):
            for (bank, c0) in ((bU, FW), (bV, 1024)):
                def mm(e, bank=bank, c0=c0):
                    ins = None
                    for k in range(8):
                        ins = e.matmul(bkf[bank], lhsT=hT[i % 2][:, k * 128:(k + 1) * 128], rhs=w_in_sl(k, c0, c0 + 512), start=(k == 0), stop=(k == 7))
                    return ins
                P.op(PE, mm, reads=[BhT[i % 2]], writes=[BK[bank]])

        def s_gelu(i):
            P.op(ACT, (lambda e: e.activation(out=vf[i % 3], in_=bkf[bV], func=AF.Gelu_apprx_tanh)), reads=[BK[bV]], writes=[Bvf[i % 3]])
            P.op(ACT, (lambda e: e.activation(out=uf[i % 5], in_=bkf[bU], func=AF.Gelu_apprx_tanh)), reads=[BK[bU]], writes=[Buf_[i % 5]])

        def s_vsq(i):
            s_ = st[i % NS]
            P.op(ACT, (lambda e: e.activation(out=sq[i % 2], in_=vf[i % 3], func=AF.Square)), reads=[Bvf[i % 3]], writes=[Bsq[i % 2]])
            P.op(DVE, (lambda e: e.tensor_reduce(out=s_[:, 4:8], in_=sq[i % 2].rearrange("p (h n) -> p h n", h=4), axis=mybir.AxisListType.X, op=ALU.add)), reads=[Bsq[i % 2]], writes=[Bst[i % NS]])
            rstd_pool(s_[:, 4:8], s_[:, 8:12], 4, 128, 1.0 / 128, Bst[i % NS])

        def s_vn(i):
            s_ = st[i % NS]
            for hd in range(4):
                P.op(DVE, (lambda e, hd=hd: e.scalar_tensor_tensor(out=vn[i % 2][:, hd * 128:(hd + 1) * 128], in0=vf[i % 3][:, hd * 128:(hd + 1) * 128], scalar=s_[:, 8 + hd:9 + hd], in1=sgain_bc[:, hd * 128:(hd + 1) * 128], op0=ALU.mult, op1=ALU.mult)), reads=[Bvf[i % 3], Bst[i % NS], B_con], writes=[Bvn[i % 2]])

        def s_sp(i):
            def mms(e):
                ins = None
                for hd in range(4):
                    ins = e.matmul(bkf[bS][:, hd * 128:(hd + 1) * 128], lhsT=WsT[:, hd * 128:(hd + 1) * 128], rhs=vn[i % 2][:, hd * 128:(hd + 1) * 128], start=True, stop=True)
                return ins
            P.op(PE, mms, reads=[Bvn[i % 2], B_con], writes=[BK[bS]])

        def s_gate(i):
            for hd in range(4):
                P.op(DVE, (lambda e, hd=hd: e.scalar_tensor_tensor(out=gt[i % 2][:, hd * 128:(hd + 1) * 128], in0=bkf[bS][:, hd * 128:(hd + 1) * 128], scalar=bcol[:, hd:hd + 1], in1=uf[i % 5][:, hd * 128:(hd + 1) * 128], op0=ALU.add, op1=ALU.mult)), reads=[BK[bS], Buf_[i % 5], B_con], writes=[Bgt[i % 2]])

        def s_trg(i):
            xsrc, r0, FTscr, X1, x1r0, mcol = tiles[i]
            P.dma(SP, (lambda e: e.dma_start(out=ft[i % 3].rearrange("p (g n) -> p g n", g=4), in_=FTscr[:, :, r0:r0 + 128].rearrange("g m n -> m g n"))), writes=[Bft[i % 3]])
            def tr(e):
                ins = None
                for hd in range(4):
                    ins = e.transpose(out=bkb[bG][:, hd * 128:(hd + 1) * 128], in_=gt[i % 2][:, hd * 128:(hd + 1) * 128], identity=identb)
                return ins
            P.op(PE, tr, reads=[Bgt[i % 2], B_con], writes=[BK[bG]])

        def s_evg(i):
            P.op(ACT, (lambda e: e.activation(out=gT[i % 2], in_=bkb[bG][:, 0:512], func=AF.Copy)), reads=[BK[bG]], writes=[BgT[i % 2]])

        def s_wo(i):
            def mmo(e):
                ins = None
                for nh in range(2):
                    for k in range(8):
                        l = ft[i % 3][:, k * 128:(k + 1) * 128] if k < 4 else gT[i % 2][:, (k - 4) * 128:(k - 3) * 128]
                        ins = e.matmul(bkf[bO[nh]], lhsT=l, rhs=w_out_sl(k, nh * 512, (nh + 1) * 512), start=(k == 0), stop=(k == 7))
                return ins
            P.op(PE, mmo, reads=[Bft[i % 3], BgT[i % 2]], writes=[BK[bO[0]], BK[bO[1]]])

        def s_ocp(i):
            xsrc, r0, FTscr, X1, x1r0, mcol = tiles[i]
            ob = i % 3
            P.dma(SP, (lambda e: e.dma_start(out=xr[i % 3], in_=xsrc[r0:r0 + 128, :])), writes=[Bxr[i % 3]])
            P.op(ACT, (lambda e: e.activation(out=tmp[ob][:, 0:512], in_=bkf[bO[0]], func=AF.Copy)), reads=[BK[bO[0]]], writes=[Btmp[ob]])
            P.op(DVE, (lambda e: e.tensor_copy(out=tmp[ob][:, 512:1024], in_=bkf[bO[1]])), reads=[BK[bO[1]]], writes=[Btmp2[ob]])

        def s_osq(i):
            s_ = st[i % NS]
            ob = i % 3
            P.op(ACT, (lambda e: e.activation(out=junk2, in_=tmp[ob], func=AF.Square, accum_out=s_[:, 14:15])), reads=[Btmp[ob], Btmp2[ob]], writes=[Bjunk2, Bst[i % NS]])
            rstd_pool(s_[:, 14:15], s_[:, 15:16], 1, 128, 1.0 / D, Bst[i % NS])

        def s_out(i):
            xsrc, r0, FTscr, X1, x1r0, mcol = tiles[i]
            s_ = st[i % NS]
            ob = i % 3
            P.op(DVE, (lambda e: e.scalar_tensor_tensor(out=tmp[ob], in0=tmp[ob], scalar=s_[:, 15:16], in1=gpm_bc, op0=ALU.mult, op1=ALU.mult)), reads=[Btmp[ob], Btmp2[ob], Bst[i % NS], B_con], writes=[Btmp[ob], Btmp2[ob]])
            P.op(DVE, (lambda e: e.tensor_tensor(out=tmp[ob], in0=tmp[ob], in1=xr[i % 3], op=ALU.add)), reads=[Btmp[ob], Btmp2[ob], Bxr[i % 3]], writes=[Btmp[ob], Btmp2[ob]])
            if mcol is not None:
                P.op(DVE, (lambda e: e.tensor_scalar(out=tmp[ob], in0=tmp[ob], scalar1=maskt[:, mcol:mcol + 1], scalar2=None, op0=ALU.mult)), reads=[Btmp[ob], Btmp2[ob], B_con], writes=[Btmp[ob], Btmp2[ob]])
            P.dma(POOL, (lambda e: e.dma_start(out=X1[x1r0:x1r0 + 128, :], in_=tmp[ob])), reads=[Btmp[ob], Btmp2[ob]], writes=[B_X1])

        run_pipeline(NT, [s_load, s_sq, s_scale, s_tr, s_evh, s_uv, s_gelu, s_vsq, s_vn, s_sp, s_gate, s_trg, s_evg, s_wo, s_ocp, s_osq, s_out])
        P.barrier()
        REG.reset(mk_r); WRK.reset(mk_w)

    units = []
    for i in range(S_s // 512):
        units.append((xs, i * 512, FT_s, X1_s, 1 + i * 512, 4, None))
    units.append((xpo, 0, FT_p, X1_p, 0, 1, 0))
    for i in range(OWN // 512):
        units.append((xpo, 128 + i * 512, FT_p, X1_p, 128 + i * 512, 4, None))
    units.append((xpo, 128 + OWN, FT_p, X1_p, 128 + OWN, 1, 1))
    mixer_pass(units)

    REG.reset(0)
    w_up_b = REG.alloc(8 * 2 * DFF)
    w_dn_b = REG.alloc(NJ * D)
    load_weight(w_up_b, w_up, 8, 2 * DFF, scale_col0=8, stage_cols=1408)

    w_down_dram = w_down

    def ffn_pass(funits):
        xa = WRK.alloc(D, F32); Bxa = Buf()
        xr = [WRK.alloc(D, F32) for _ in range(2)]; Bxr = [Buf(), Buf()]
        otmp = WRK.alloc(D, F32); Bot = Buf(); Bot2 = Buf()
        mk_t = WRK.mark()
        trr = [WRK.alloc(256, F32) for _ in range(2)]; Btrr = [Buf(), Buf()]
        junk5 = WRK.ap[:, mk_t // 2:mk_t // 2 + 1024].bitcast(F32)
        hb = [WRK.alloc(D) for _ in range(2)]; Bhb = [Buf(), Buf()]
        h2T = WRK.alloc(8 * 514); Bh2T = [Buf() for _ in range(5)]
        pT = WRK.alloc(NJ * 512); BpT = [[Buf(), Buf()] for _ in range(NJ)]
        mk_j = WRK.mark()
        tg = [WRK.alloc(256, F32) for _ in range(2)]; Btg = [Buf(), Buf()]
        tu = [WRK.alloc(256, F32) for _ in range(2)]; Btu = [Buf(), Buf()]
        junkf = WRK.ap[:, mk_j // 2:mk_j // 2 + 2048].bitcast(F32)
        Bjunkf = Btg + Btu
        gg = [WRK.alloc(256) for _ in range(2)]; Bgg = [Buf(), Buf()]
        st = [WRK.alloc(8, F32) for _ in range(4)]; Bst = [Buf() for _ in range(4)]
        h2T3 = h2T.rearrange("p (k n) -> p k n", k=8)
        pT3 = pT.rearrange("p (j n) -> p j n", j=NJ)
        B_Y = Buf("yout")
        cnt = {"p": 0, "o": 0}
        B_wdn = [Buf() for _ in range(NJ)]

        def wdn_task(j):
            stg, Bs_ = ((otmp, [Bot, Bot2]), (xr[0], [Bxr[0]]), (xr[1], [Bxr[1]]))[j % 3]
            P.dma(SP, (lambda e: e.dma_start(out=stg, in_=w_down_dram[j * 128:(j + 1) * 128, :])), writes=Bs_)
            P.op(DVE, (lambda e: e.tensor_copy(out=w_dn_b[:, j * D:(j + 1) * D], in_=stg)), reads=Bs_, writes=[B_wdn[j]])

        pst = {}

        def prep_a(unit, t):
            X1, r0, ydst, y0 = unit
            q = cnt["p"]; cnt["p"] += 1
            hbi, s_, Bs, pb = q % 2, st[q % 4], Bst[q % 4], q % 2
            if t < 4:
                np_ = 128
                xsrc_, Bxs_ = xa, Bxa
                P.dma(SP, (lambda e: e.dma_start(out=xa, in_=X1[r0 + t * 128:r0 + (t + 1) * 128, :])), writes=[Bxa])
            else:
                np_ = 2
                xsrc_, Bxs_ = xr[1], Bxr[1]
                P.dma(SP, (lambda e: e.dma_start(out=xr[1][0:1, :], in_=X1[r0 - 1:r0, :])), writes=[Bxr[1]])
                P.dma(SP, (lambda e: e.dma_start(out=xr[1][1:2, :], in_=X1[r0 + 512:r0 + 513, :])), writes=[Bxr[1]])
            norm_to_bf16(xsrc_[0:np_, :], np_, junkf, s_[:, 0:1], s_[:, 1:2], hb[hbi], Bxs_, Bs, Bhb[hbi], Bjunkf)
            pst[(id(unit), t)] = (hbi, pb, np_)

        def prep_b(unit, t):
            hbi, pb, np_ = pst.pop((id(unit), t))
            def tr(e):
                ins = None
                for k in range(8):
                    ins = e.transpose(out=bkb[pb][:, k * np_:(k + 1) * np_], in_=hb[hbi][0:np_, k * 128:(k + 1) * 128], identity=identb[0:np_, 0:np_])
                return ins
            P.op(PE, tr, reads=[Bhb[hbi], B_con], writes=[BK[pb]])
            if t < 4:
                dst = h2T3[:, :, 1 + t * 128:1 + (t + 1) * 128]
                srcv = bkb[pb].rearrange("p (k n) -> p k n", k=8)
                P.op(ACT, (lambda e: e.activation(out=dst, in_=srcv, func=AF.Copy)), reads=[BK[pb]], writes=[Bh2T[t]])
            else:
                srcv = bkb[pb][:, 0:16].rearrange("p (k n) -> p k n", k=8)
                P.op(ACT, (lambda e: e.activation(out=h2T3[:, :, 0:1], in_=srcv[:, :, 0:1], func=AF.Copy)), reads=[BK[pb]], writes=[Bh2T[t]])
                P.op(ACT, (lambda e: e.activation(out=h2T3[:, :, 513:514], in_=srcv[:, :, 1:2], func=AF.Copy)), reads=[BK[pb]], writes=[Bh2T[t]])

        def w_up_phase(extra=None):
            def bk_of(c):
                return ((0, 1), (2, 3), (4, 5))[c % 3]

            def wu0(c):
                j, c0 = c % NJ, (c // NJ) * 256
                pg, pu = bk_of(c)
                def mmg(e):
                    ins = None
                    for k in range(8):
                        ins = e.matmul(bkf[pg][:, 0:258], lhsT=w_up_b[:, k * 2 * DFF + j * 128:k * 2 * DFF + (j + 1) * 128], rhs=h2T3[:, k, c0:c0 + 258], start=(k == 0), stop=(k == 7))
                    return ins
                def mmup(e):
                    ins = None
                    for k in range(8):
                        ins = e.matmul(bkf[pu][:, 0:258], lhsT=w_up_b[:, k * 2 * DFF + DFF + j * 128:k * 2 * DFF + DFF + (j + 1) * 128], rhs=h2T3[:, k, c0:c0 + 258], start=(k == 0), stop=(k == 7))
                    return ins
                P.op(PE, mmg, reads=Bh2T, writes=[BK[pg]])
                P.op(PE, mmup, reads=Bh2T, writes=[BK[pu]])

            def wu1(c):
                j, cb = c % NJ, c % 2
                pg, pu = bk_of(c)
                jg, ju = j, NJ + j
                cw = lambda t, jj: cwT[:, t * 44 + jj:t * 44 + jj + 1]
                P.op(ACT, (lambda e: e.activation(out=tg[cb], in_=bkf[pg][:, 1:257], func=AF.Identity, scale=cw(1, jg), bias=cbT[:, jg:jg + 1])), reads=[BK[pg], B_con], writes=[Btg[cb]])
                P.op(ACT, (lambda e: e.activation(out=tu[cb], in_=bkf[pu][:, 1:257], func=AF.Identity, scale=cw(1, ju), bias=cbT[:, ju:ju + 1])), reads=[BK[pu], B_con], writes=[Btu[cb]])
                P.op(ACT, (lambda e: e.activation(out=trr[cb], in_=bkf[pu][:, 2:258], func=AF.Copy, scale=cw(2, ju))), reads=[BK[pu], B_con], writes=[Btrr[cb]])
                P.op(DVE, (lambda e: e.scalar_tensor_tensor(out=tg[cb], in0=bkf[pg][:, 0:256], scalar=cw(0, jg), in1=tg[cb], op0=ALU.mult, op1=ALU.add)), reads=[BK[pg], B_con, Btg[cb]], writes=[Btg[cb]])
                P.op(DVE, (lambda e: e.scalar_tensor_tensor(out=tg[cb], in0=bkf[pg][:, 2:258], scalar=cw(2, jg), in1=tg[cb], op0=ALU.mult, op1=ALU.add)), reads=[BK[pg], B_con, Btg[cb]], writes=[Btg[cb]])
                P.op(DVE, (lambda e: e.scalar_tensor_tensor(out=tu[cb], in0=bkf[pu][:, 0:256], scalar=cw(0, ju), in1=tu[cb], op0=ALU.mult, op1=ALU.add)), reads=[BK[pu], B_con, Btu[cb]], writes=[Btu[cb]])

            def wu2(c):
                j, cb, half = c % NJ, c % 2, c // NJ
                c0 = half * 256
                P.op(ACT, (lambda e: e.activation(out=gg[cb], in_=tg[cb], func=AF.Gelu_apprx_tanh)), reads=[Btg[cb]], writes=[Bgg[cb]])
                P.op(POOL, (lambda e: e.tensor_tensor(out=trr[cb], in0=tu[cb], in1=trr[cb], op=ALU.add)), reads=[Btu[cb], Btrr[cb]], writes=[Btrr[cb]])
                P.op(POOL, (lambda e: e.tensor_tensor(out=pT3[:, j, c0:c0 + 256], in0=gg[cb], in1=trr[cb], op=ALU.mult)), reads=[Bgg[cb], Btrr[cb]], writes=[BpT[j][half]])

            n = 2 * NJ
            for it in range(n + 2):
                if extra is not None:
                    extra(it)
                if it < n:
                    wu0(it)
                if 0 <= it - 1 < n:
                    wu1(it - 1)
                if 0 <= it - 2 < n:
                    wu2(it - 2)

        def w_down(unit, t, nh):
            def mmd(e):
                ins = None
                for j in range(NJ):
                    ins = e.matmul(bkf[6 + nh], lhsT=pT3[:, j, t * 128:(t + 1) * 128], rhs=w_dn_b[:, j * D + nh * 512:j * D + (nh + 1) * 512], start=(j == 0), stop=(j == NJ - 1))
                return ins
            P.op(PE, mmd, reads=[BpT[j][t // 2] for j in range(NJ)] + B_wdn, writes=[BK[6 + nh]])

        def post_a(unit, t, nh):
            X1, r0, ydst, y0 = unit
            xb = t % 2
            if nh == 0:
                P.dma(SP, (lambda e: e.dma_start(out=xr[xb], in_=X1[r0 + t * 128:r0 + (t + 1) * 128, :])), writes=[Bxr[xb]])
                P.op(DVE, (lambda e: e.tensor_copy(out=otmp[:, 0:512], in_=bkf[6])), reads=[BK[6]], writes=[Bot])
            else:
                P.op(ACT, (lambda e: e.activation(out=otmp[:, 512:1024], in_=bkf[7], func=AF.Copy)), reads=[BK[7]], writes=[Bot2])

        def post_b(unit, t):
            X1, r0, ydst, y0 = unit
            q = cnt["o"]; cnt["o"] += 1
            s_, Bs = st[q % 4], Bst[q % 4]
            xb = t % 2
            P.op(ACT, (lambda e: e.activation(out=junkf, in_=otmp, func=AF.Square, accum_out=s_[:, 4:5])), reads=[Bot, Bot2], writes=Bjunkf + [Bs])
            rstd_pool(s_[:, 4:5], s_[:, 5:6], 1, 128, 1.0 / D, Bs)
            P.op(DVE, (lambda e: e.scalar_tensor_tensor(out=otmp, in0=otmp, scalar=s_[:, 5:6], in1=gpf_bc, op0=ALU.mult, op1=ALU.mult)), reads=[Bot, Bot2, Bs, B_con], writes=[Bot, Bot2])
            P.op(POOL, (lambda e: e.tensor_tensor(out=xr[xb], in0=otmp, in1=xr[xb], op=ALU.add)), reads=[Bot, Bot2, Bxr[xb]], writes=[Bxr[xb]])
            P.dma(POOL, (lambda e: e.dma_start(out=ydst[y0 + t * 128:y0 + (t + 1) * 128, :], in_=xr[xb])), reads=[Bxr[xb]], writes=[B_Y])

        for t in range(5):
            prep_a(funits[0], t)
            prep_b(funits[0], t)
        for ui, unit in enumerate(funits):
            nxt = funits[ui + 1] if ui + 1 < len(funits) else None
            if ui == 0:
                w_up_phase(extra=(lambda it: wdn_task(it // 2) if (it % 2 == 0 and it // 2 < NJ) else None))
            else:
                w_up_phase()
            if nxt is not None:
                prep_a(nxt, 4)
                prep_a(nxt, 0)
            for t in range(4):
                w_down(unit, t, 0)
                post_a(unit, t, 0)
                if nxt is not None and t == 3:
                    prep_b(nxt, 3)
                w_down(unit, t, 1)
                post_a(unit, t, 1)
                if nxt is not None and t < 3:
                    prep_b(nxt, t)
                    if t == 0:
                        prep_b(nxt, 4)
                    prep_a(nxt, t + 1)
                post_b(unit, t)
        P.barrier()

    funits = []
    for i in range(S_s // 512):
        funits.append((X1_s, 1 + i * 512, ys, i * 512))
    for i in range(OWN // 512):
        funits.append((X1_p, 128 + i * 512, yp, i * 512))
    ffn_pass(funits)
    P.emit()
    return nc


def _bf(a):
    return np.ascontiguousarray(a.astype(np.float32)).astype(ml_dtypes.bfloat16)


def _tables(cfg):
    out = {}
    for name, S, NF in (("s", cfg.S_s, cfg.NF_s), ("p", cfg.S_p, cfg.NF_p)):
        sf = np.arange(NF)[:, None, None]
        ss_ = np.arange(128)[None, :, None]
        kf = np.arange(128)[None, None, :]
        s = (NF * ss_ + sf).astype(np.int64)
        ph = ((kf * s) % S).astype(np.float64) * (2 * np.pi / S)
        tab = np.stack([np.cos(ph)[:, :, 0:KH], -np.sin(ph)[:, :, 0:KH]], axis=2)
        out["tabA_" + name] = _bf(np.ascontiguousarray(tab.transpose(1, 0, 2, 3)).reshape(128, NF * 2 * KH))
        c = np.arange(128)[:, None]
        m = np.arange(128)[None, :]
        ph2 = ((c * m) % 128).astype(np.float64) * (2 * np.pi / 128)
        sc = 1.0 / np.sqrt(S * 128.0)
        out["cs_" + name] = _bf(np.stack([np.cos(ph2) * sc, np.sin(ph2) * sc], axis=1))
    return out


def _tabC(NF, ks_vals, mirror=False):
    sf = np.arange(NF)[:, None]
    ks = np.asarray(ks_vals, dtype=np.int64)[None, :] + (1 if mirror else 0)
    ph = ((sf * ks) % NF).astype(np.float64) * (2 * np.pi / NF)
    c, s = np.cos(ph), np.sin(ph)
    if not mirror:
        t0 = np.concatenate([c, -s], axis=1)
        t1 = np.concatenate([s, c], axis=1)
    else:
        t0 = np.concatenate([c, -s], axis=1)
        t1 = np.concatenate([-s, -c], axis=1)
    return _bf(np.stack([t0, t1], axis=1))


_CACHE = {}


def kernel_impl(cfg, x_prompt, x_sample, pre_mix_norm, w_in, sgu_norm, w_spatial, b_spatial, w_out,
                post_mix_norm, pre_ffn_norm, w_up, conv_w, conv_b, w_down, post_ffn_norm):
    f = lambda a: np.ascontiguousarray(np.asarray(a, dtype=np.float32))
    x_prompt, x_sample = f(x_prompt), f(x_sample)
    S_s, S_p, OWN, EXT = cfg.S_s, cfg.S_p, cfg.OWN, cfg.EXT
    key = (S_s, S_p, cfg.debug)
    if key not in _CACHE:
        _CACHE[key] = build_program(cfg)
    nc = _CACHE[key]
    tabs = _tables(cfg)
    common = dict(
        xpf=np.ascontiguousarray(x_prompt[0].reshape(128, cfg.NF_p, D).transpose(1, 0, 2)).reshape(S_p, D),
        g_pre=np.ascontiguousarray(np.stack([f(pre_mix_norm)[0].reshape(8, 128), f(pre_ffn_norm)[0].reshape(8, 128)])),
        w_in=f(w_in)[0], sgain=f(sgu_norm)[0].reshape(512), w_sp=f(w_spatial)[0], b_sp=f(b_spatial)[0].reshape(512),
        w_out=f(w_out)[0], g_pm=f(post_mix_norm)[0], w_up=f(w_up)[0],
        conv_w=np.ascontiguousarray(f(conv_w)[0].reshape(3 * 44, 128)), conv_b=np.ascontiguousarray(f(conv_b)[0].reshape(44, 128)),
        w_down=f(w_down)[0], g_pf=f(post_ffn_norm)[0],
        identb=_bf(np.eye(128)), identf=np.eye(128, dtype=np.float32),
        tabA_s=tabs["tabA_s"], tabA_p=tabs["tabA_p"], cs_s=tabs["cs_s"], cs_p=tabs["cs_p"],
        tabC_s=_tabC(cfg.NF_s, np.arange(cfg.NF_s)), tabM_s=_tabC(cfg.NF_s, np.arange(cfg.NF_s), mirror=True),
    )
    in_maps = []
    for j in range(NCORES):
        xpo = np.zeros((EXT, D), np.float32)
        lo, hi = j * OWN - 128, (j + 1) * OWN + 128
        a, b = max(lo, 0), min(hi, S_p)
        xpo[a - lo:b - lo] = x_prompt[0, a:b]
        mask = np.zeros((128, 2), np.float32)
        mask[:, 0] = 1.0 if j > 0 else 0.0
        mask[:, 1] = 1.0 if j < NCORES - 1 else 0.0
        ks0 = j * (OWN // 128) - 1
        m = dict(common)
        m.update(xs=x_sample[j], xst=np.ascontiguousarray(x_sample[j].reshape(128, cfg.NF_s, D).transpose(1, 0, 2)).reshape(S_s, D), xpo=xpo, mask=mask, tabC_p=_tabC(cfg.NF_p, np.arange(ks0, ks0 + cfg.NKS_p)),
                 tabM_p=_tabC(cfg.NF_p, np.arange(ks0, ks0 + cfg.NKS_p), mirror=True))
        in_maps.append(m)
    res = run_bass_kernel_spmd(nc, in_maps, core_ids=list(range(NCORES)))
    y_prompt = np.concatenate([r["yp"] for r in res.results], axis=0)[None]
    y_sample = np.stack([r["ys"] for r in res.results], axis=0)
    if cfg.debug:
        return (y_prompt.astype(np.float32), y_sample.astype(np.float32)), res.results
    return (y_prompt.astype(np.float32), y_sample.astype(np.float32))


def kernel(**inputs):
    return kernel_impl(Cfg(), **inputs)
```

```python
from contextlib import ExitStack
import numpy as np
import ml_dtypes
import concourse.bass as bass
import concourse.mybir as mybir
from concourse.bass_utils import run_bass_kernel_spmd

F32 = mybir.dt.float32
BF16 = mybir.dt.bfloat16
AF = mybir.ActivationFunctionType
ALU = mybir.AluOpType

D = 1024
FW = 512
INC = 1536
DFF = 2816
NJ = DFF // 128
EPS = 1e-6
NCORES = 8
KH = 65

PE, ACT, DVE, POOL, SP = "tensor", "scalar", "vector", "gpsimd", "sync"
ENGS = (PE, ACT, DVE, POOL, SP)
DMA_SLOTS = 8
SAME_ENGINE_SYNC = True


class Buf:
    __slots__ = ("name", "w", "r", "excl")

    def __init__(self, name="", excl=False):
        self.name = name
        self.w = None
        self.r = {}
        self.excl = excl


class Prog:
    def __init__(self, nc):
        self.nc = nc
        self.ops = {e: [] for e in ENGS}
        self.cnt = {e: 0 for e in ENGS}
        self.known = {e: {} for e in ENGS}
        self.dma_n = {e: 0 for e in ENGS}
        self.semkeys = set()

    def _deps(self, eng, reads, writes, extra=(), own=None):
        need = {}

        def add(t):
            if t is None:
                return
            k, v = t
            if need.get(k, 0) < v:
                need[k] = v

        for b in reads:
            add(b.w)
            if b.excl:
                for k, v in b.r.items():
                    if k != own:
                        add((k, v))
        for b in writes:
            add(b.w)
            for k, v in b.r.items():
                add((k, v))
        for t in extra:
            add(t)
        out = []
        kn = self.known[eng]
        for k, v in need.items():
            if (not SAME_ENGINE_SYNC) and k == ("c", eng):
                continue
            if kn.get(k, 0) >= v:
                continue
            kn[k] = v
            out.append((k, v))
        return out

    def _mark(self, tok, reads, writes):
        k, v = tok
        for b in reads:
            if b.r.get(k, 0) < v:
                b.r[k] = v
        for b in writes:
            b.w = tok
            b.r = {}

    def op(self, eng, fn, reads=(), writes=()):
        waits = self._deps(eng, reads, writes, own=("c", eng))
        self.cnt[eng] += 1
        key = ("c", eng)
        self.semkeys.add(key)
        tok = (key, self.cnt[eng])
        self.ops[eng].append((waits, fn, key, 1))
        self._mark(tok, reads, writes)
        return tok

    def dma(self, q, fn, reads=(), writes=()):
        n = self.dma_n[q]
        self.dma_n[q] += 1
        slot = n % DMA_SLOTS
        key = ("d", q, slot)
        self.semkeys.add(key)
        val = 16 * (n // DMA_SLOTS + 1)
        extra = [(key, val - 16)] if val > 16 else []
        waits = self._deps(q, reads, writes, extra)
        tok = (key, val)
        self.ops[q].append((waits, fn, key, 16))
        self._mark(tok, reads, writes)
        return tok

    def barrier(self, engs=ENGS):
        allt = []
        for e in ENGS:
            if self.cnt[e]:
                allt.append((("c", e), self.cnt[e]))
            n = self.dma_n[e]
            for s in range(min(n, DMA_SLOTS)):
                cntv = (n - 1 - s) // DMA_SLOTS + 1
                allt.append((("d", e, s), 16 * cntv))
        for e in engs:
            waits = self._deps(e, (), (), allt)
            if waits:
                self.ops[e].append((waits, None, None, 0))

    def emit(self):
        nc = self.nc
        with ExitStack() as es:
            sems = {}
            for k in sorted(self.semkeys, key=str):
                sems[k] = es.enter_context(nc.semaphore("s_" + "_".join(str(x) for x in k)))
            block = es.enter_context(nc.Block())

            def make(eng):
                def body(e):
                    for waits, fn, key, inc in self.ops[eng]:
                        for k, v in waits:
                            e.wait_ge(sems[k], v)
                        if fn is not None:
                            fn(e).then_inc(sems[key], inc)
                return body

            for eng in ENGS:
                if self.ops[eng]:
                    getattr(block, eng)(make(eng))


class Arena:
    def __init__(self, nc, name, nbytes):
        self.t = nc.alloc_sbuf_tensor(name, [128, nbytes // 2], BF16)
        self.ap = self.t[:]
        self.off = 0
        self.cap = nbytes

    def mark(self):
        return self.off

    def reset(self, m):
        self.off = m

    def alloc(self, n_elems, dtype=BF16):
        esz = 4 if dtype == F32 else 2
        nb = (n_elems * esz + 63) // 64 * 64
        assert self.off + nb <= self.cap, f"arena overflow {self.off}+{nb}>{self.cap}"
        a = self.ap[:, self.off // 2:(self.off + nb) // 2]
        self.off += nb
        if dtype == F32:
            a = a.bitcast(F32)
        return a[:, 0:n_elems]


class Cfg:
    def __init__(self, S_s=8192, S_p=16384, debug=False):
        self.S_s = S_s
        self.S_p = S_p
        self.OWN = S_p // NCORES
        self.EXT = self.OWN + 256
        self.NF_s = S_s // 128
        self.NF_p = S_p // 128
        self.NKS_p = self.OWN // 128 + 2
        self.debug = debug
        assert S_s % 512 == 0 and self.OWN % 512 == 0


def build_program(cfg):
    nc = bass.Bass("TRN2", target_bir_lowering=False)
    P = Prog(nc)
    dbg = cfg.debug

    def din(name, shape, dt=F32):
        return nc.dram_tensor(name, list(shape), dt, kind="ExternalInput").ap()

    def dout(name, shape, dt=F32):
        return nc.dram_tensor(name, list(shape), dt, kind="ExternalOutput").ap()

    def dscr(name, shape, dt):
        kind = "ExternalOutput" if dbg else "Internal"
        return nc.dram_tensor(name, list(shape), dt, kind=kind).ap()

    S_s, S_p, OWN, EXT = cfg.S_s, cfg.S_p, cfg.OWN, cfg.EXT
    xs = din("xs", [S_s, D])
    xpf = din("xpf", [S_p, D])
    xst = din("xst", [S_s, D])
    xpo = din("xpo", [EXT, D])
    maskd = din("mask", [128, 2])
    g_pre = din("g_pre", [2, 8, 128])
    w_in = din("w_in", [D, INC])
    sgain = din("sgain", [512])
    w_sp = din("w_sp", [4, 128, 128])
    b_sp = din("b_sp", [512])
    w_out = din("w_out", [D, D])
    g_pm = din("g_pm", [D])
    w_up = din("w_up", [D, 2 * DFF])
    conv_w = din("conv_w", [132, 128])
    conv_b = din("conv_b", [44, 128])
    w_down = din("w_down", [DFF, D])
    g_pf = din("g_pf", [D])
    identb_d = din("identb", [128, 128], BF16)
    identf_d = din("identf", [128, 128])
    tabA_s = din("tabA_s", [128, cfg.NF_s * 2 * KH], BF16)
    tabA_p = din("tabA_p", [128, cfg.NF_p * 2 * KH], BF16)
    tabC_s = din("tabC_s", [cfg.NF_s, 2, 2 * cfg.NF_s], BF16)
    tabC_p = din("tabC_p", [cfg.NF_p, 2, 2 * cfg.NKS_p], BF16)
    tabM_s = din("tabM_s", [cfg.NF_s, 2, 2 * cfg.NF_s], BF16)
    tabM_p = din("tabM_p", [cfg.NF_p, 2, 2 * cfg.NKS_p], BF16)
    cs_s = din("cs_s", [128, 2, 128], BF16)
    cs_p = din("cs_p", [128, 2, 128], BF16)
    ys = dout("ys", [S_s, D])
    yp = dout("yp", [OWN, D])
    A_s = dscr("A_s", [KH, cfg.NF_s, 2 * FW], BF16)
    A_p = dscr("A_p", [KH, cfg.NF_p, 2 * FW], BF16)
    FT_s = dscr("FT_s", [4, 128, S_s], BF16)
    FT_p = dscr("FT_p", [4, 128, EXT], BF16)
    X1_s = dscr("X1_s", [S_s + 2, D], F32)
    X1_p = dscr("X1_p", [EXT, D], F32)

    banks = [nc.alloc_psum_tensor(f"bank{i}", [128, 512], F32) for i in range(8)]
    bkf = [b[:] for b in banks]
    bkb = [b[:].bitcast(BF16) for b in banks]
    BK = [Buf(f"bank{i}", excl=True) for i in range(8)]

    CON = Arena(nc, "con", 16896)
    REG = Arena(nc, "reg", 132 * 1024)
    WRK = Arena(nc, "wrk", 58 * 1024)

    identb = CON.alloc(128); identf = CON.alloc(128, F32)
    gcol = CON.alloc(16, F32)
    cwT = CON.alloc(132, F32)
    cbT = CON.alloc(44, F32)
    maskt = CON.alloc(2, F32)
    epst = CON.alloc(1, F32)
    mhalf = CON.alloc(4, F32)
    bcol = CON.alloc(4, F32)
    gpm_bc = CON.alloc(D, F32); gpf_bc = CON.alloc(D, F32)
    sgain_bc = CON.alloc(512, F32); bsp_bc = CON.alloc(512, F32)
    WsT = CON.alloc(512)
    cs128 = [CON.alloc(256), CON.alloc(256)]
    B_con = Buf("con")

    def ld(dst, src, q=SP, bufs=(B_con,)):
        P.dma(q, lambda e: e.dma_start(out=dst, in_=src), writes=list(bufs))

    ld(identb, identb_d[:, :]); ld(identf, identf_d[:, :]); ld(maskt, maskd[:, :])
    ld(gpm_bc, g_pm.partition_broadcast(128)); ld(gpf_bc, g_pf.partition_broadcast(128))
    ld(sgain_bc, sgain.partition_broadcast(128)); ld(bsp_bc, b_sp.partition_broadcast(128))
    ld(cs128[0], cs_s.rearrange("c t m -> c (t m)")); ld(cs128[1], cs_p.rearrange("c t m -> c (t m)"))
    P.op(DVE, lambda e: e.memset(epst, EPS), writes=[B_con])
    P.op(DVE, lambda e: e.memset(mhalf, -0.5), writes=[B_con])
    m0 = WRK.mark()
    zrow = WRK.alloc(D, F32)
    P.op(DVE, lambda e: e.memset(zrow, 0.0), writes=[B_con])
    B_x1s = Buf("x1s_dram")
    P.dma(SP, lambda e: e.dma_start(out=X1_s[0:1, :], in_=zrow[0:1, :]), reads=[B_con], writes=[B_x1s])
    P.dma(SP, lambda e: e.dma_start(out=X1_s[S_s + 1:S_s + 2, :], in_=zrow[0:1, :]), reads=[B_con], writes=[B_x1s])

    st_rows = WRK.alloc(128, F32); st_rows2 = WRK.alloc(128, F32); st_rows3 = WRK.alloc(128, F32); st_rows4 = WRK.alloc(128, F32); st_rows5 = WRK.alloc(128, F32)
    B_st = Buf("st")
    ld(st_rows[0:16, :], g_pre.rearrange("a k p -> (a k) p"), bufs=(B_st,))
    ld(st_rows2[0:128, :], conv_w[0:128, :], bufs=(B_st,))
    ld(st_rows3[0:4, :], conv_w[128:132, :], bufs=(B_st,))
    ld(st_rows4[0:44, :], conv_b[:, :], bufs=(B_st,))
    ld(st_rows5[0:4, :], b_sp.rearrange("(h p) -> h p", h=4), bufs=(B_st,))

    def trf(e):
        e.transpose(out=bkf[0][:, 0:16], in_=st_rows[0:16, :], identity=identf[0:16, 0:16])
        e.transpose(out=bkf[0][:, 16:144], in_=st_rows2[0:128, :], identity=identf[:, :])
        e.transpose(out=bkf[0][:, 144:148], in_=st_rows3[0:4, :], identity=identf[0:4, 0:4])
        e.transpose(out=bkf[0][:, 148:192], in_=st_rows4[0:44, :], identity=identf[0:44, 0:44])
        return e.transpose(out=bkf[0][:, 192:196], in_=st_rows5[0:4, :], identity=identf[0:4, 0:4])
    P.op(PE, trf, reads=[B_st, B_con], writes=[BK[0]])
    P.op(DVE, lambda e: e.tensor_copy(out=gcol, in_=bkf[0][:, 0:16]), reads=[BK[0]], writes=[B_con])
    P.op(DVE, lambda e: e.tensor_copy(out=cwT, in_=bkf[0][:, 16:148]), reads=[BK[0]], writes=[B_con])
    P.op(DVE, lambda e: e.tensor_copy(out=cbT, in_=bkf[0][:, 148:192]), reads=[BK[0]], writes=[B_con])
    P.op(DVE, lambda e: e.tensor_copy(out=bcol, in_=bkf[0][:, 192:196]), reads=[BK[0]], writes=[B_con])
    wsp_f = WRK.alloc(512, F32); wsp_b = WRK.alloc(512)
    ld(wsp_f, w_sp.rearrange("h p q -> p h q"), bufs=(B_st,))
    P.op(DVE, lambda e: e.tensor_copy(out=wsp_b, in_=wsp_f), reads=[B_st], writes=[B_st])

    def trw(e):
        ins = None
        for hd in range(4):
            ins = e.transpose(out=bkb[1][:, hd * 128:(hd + 1) * 128], in_=wsp_b[:, hd * 128:(hd + 1) * 128], identity=identb)
        return ins
    P.op(PE, trw, reads=[B_st, B_con], writes=[BK[1]])
    P.op(DVE, lambda e: e.tensor_copy(out=WsT, in_=bkb[1][:, 0:512]), reads=[BK[1]], writes=[B_con])
    P.barrier()
    WRK.reset(m0)

    def load_weight(dst3, src, kchunks, ncols, scale_col0=None, stage_cols=1536):
        mk = WRK.mark()
        NSTG = 6
        stg = [WRK.alloc(stage_cols, F32) for _ in range(NSTG)]
        B_stg = [Buf() for _ in range(NSTG)]
        B_w = Buf("w")
        i = 0
        for k in range(kchunks):
            for c0 in range(0, ncols, stage_cols):
                c1 = min(ncols, c0 + stage_cols)
                s = stg[i % NSTG]; bs = B_stg[i % NSTG]
                P.dma(SP, (lambda e, s=s, k=k, c0=c0, c1=c1: e.dma_start(out=s[:, 0:c1 - c0], in_=src[k * 128:(k + 1) * 128, c0:c1])), writes=[bs])
                dsl = dst3[:, k * ncols + c0:k * ncols + c1]
                eng = DVE if i % 2 == 0 else ACT
                if scale_col0 is None:
                    if eng == DVE:
                        P.op(DVE, (lambda e, s=s, dsl=dsl, n=c1 - c0: e.tensor_copy(out=dsl, in_=s[:, 0:n])), reads=[bs], writes=[B_w])
                    else:
                        P.op(ACT, (lambda e, s=s, dsl=dsl, n=c1 - c0: e.activation(out=dsl, in_=s[:, 0:n], func=AF.Copy)), reads=[bs], writes=[B_w])
                else:
                    sc = gcol[:, scale_col0 + k:scale_col0 + k + 1]
                    if eng == DVE:
                        P.op(DVE, (lambda e, s=s, dsl=dsl, n=c1 - c0, sc=sc: e.tensor_scalar(out=dsl, in0=s[:, 0:n], scalar1=sc, scalar2=None, op0=ALU.mult)), reads=[bs, B_con], writes=[B_w])
                    else:
                        P.op(ACT, (lambda e, s=s, dsl=dsl, n=c1 - c0, sc=sc: e.activation(out=dsl, in_=s[:, 0:n], func=AF.Copy, scale=sc)), reads=[bs, B_con], writes=[B_w])
                i += 1
        P.barrier()
        WRK.reset(mk)

    w_in_b = REG.alloc(8 * INC)
    w_out_b = REG.alloc(8 * D)
    load_weight(w_in_b, w_in, 8, INC, scale_col0=0)
    regA_mark = REG.mark()

    def w_in_sl(k, c0, c1):
        return w_in_b[:, k * INC + c0:k * INC + c1]

    def w_out_sl(k, c0, c1):
        return w_out_b[:, k * D + c0:k * D + c1]

    def rstd_pool(ss, rstd, n, np_, inv_n, Bss):
        P.op(POOL, lambda e: e.tensor_scalar(out=rstd[0:np_, 0:n], in0=ss[0:np_, 0:n], scalar1=inv_n, scalar2=EPS, op0=ALU.mult, op1=ALU.add), reads=[Bss], writes=[Bss])
        P.op(POOL, lambda e: e.tensor_tensor(out=rstd[0:np_, 0:n], in0=rstd[0:np_, 0:n], in1=mhalf[0:np_, 0:n], op=ALU.pow), reads=[Bss, B_con], writes=[Bss])

    def norm_to_bf16(x_ap, np_, junk, ss, rstd, hb, Bx, Bss, Bh, Bjunk, scale_eng=DVE):
        P.op(ACT, lambda e: e.activation(out=junk[0:np_, :], in_=x_ap, func=AF.Square, accum_out=ss[0:np_, :]), reads=[Bx], writes=(list(Bjunk) if isinstance(Bjunk, (list, tuple)) else [Bjunk]) + [Bss])
        rstd_pool(ss, rstd, 1, np_, 1.0 / D, Bss)
        if scale_eng == DVE:
            P.op(DVE, lambda e: e.tensor_scalar(out=hb[0:np_, :], in0=x_ap, scalar1=rstd[0:np_, :], scalar2=None, op0=ALU.mult), reads=[Bx, Bss], writes=[Bh])
        else:
            P.op(ACT, lambda e: e.activation(out=hb[0:np_, :], in_=x_ap, func=AF.Copy, scale=rstd[0:np_, :]), reads=[Bx, Bss], writes=[Bh])

    def run_pipeline(n, stages):
        ns = len(stages)
        for it in range(n + ns - 1):
            for k in reversed(range(ns)):
                i = it - k
                if 0 <= i < n:
                    stages[k](i)

    def phase1(xsrc, S, NF, tabA, Ascr, bg_w_out=False):
        mk_r, mk_w = REG.mark(), WRK.mark()
        NX = 6
        xt = [REG.alloc(D, F32) for _ in range(NX)]
        junk = REG.alloc(D)
        hb = [REG.alloc(D) for _ in range(2)]
        hT = [REG.alloc(D) for _ in range(2)]
        zt = [REG.alloc(FW) for _ in range(2)]
        tball = REG.alloc(NF * 2 * KH)
        B_tb = Buf()
        P.dma(SP, lambda e: e.dma_start(out=tball, in_=tabA[:, :]), writes=[B_tb])
        NA = 4
        At = [REG.alloc(2 * FW) for _ in range(NA)]
        st = [WRK.alloc(2, F32) for _ in range(4)]
        Bxt = [Buf() for _ in range(NX)]; Bjunk = Buf(); Bhb = [Buf(), Buf()]
        BhT = [Buf(), Buf()]; Bzt = [Buf(), Buf()]; Btb = None
        BAt = [Buf() for _ in range(NA)]; Bst = [Buf() for _ in range(4)]
        B_A = Buf("Ascr")
        xv = xsrc.rearrange("(f i) d -> f i d", i=128)

        def s0(i):
            P.dma(SP, (lambda e: e.dma_start(out=xt[i % NX], in_=xv[i])), writes=[Bxt[i % NX]])

        def s1(i):
            s_ = st[i % 4]
            P.op(ACT, (lambda e: e.activation(out=junk, in_=xt[i % NX], func=AF.Square, accum_out=s_[:, 0:1])), reads=[Bxt[i % NX]], writes=[Bjunk, Bst[i % 4]])

        def s1b(i):
            s_ = st[i % 4]
            rstd_pool(s_[:, 0:1], s_[:, 1:2], 1, 128, 1.0 / D, Bst[i % 4])

        def s1c(i):
            s_ = st[i % 4]
            P.op(DVE, (lambda e: e.tensor_scalar(out=hb[i % 2], in0=xt[i % NX], scalar1=s_[:, 1:2], scalar2=None, op0=ALU.mult)), reads=[Bxt[i % NX], Bst[i % 4]], writes=[Bhb[i % 2]])

        def s2(i):
            b, pb = i % 2, i % 2
            def tr(e):
                ins = None
                for k in range(8):
                    ins = e.transpose(out=bkb[pb][:, k * 128:(k + 1) * 128], in_=hb[b][:, k * 128:(k + 1) * 128], identity=identb)
                return ins
            P.op(PE, tr, reads=[Bhb[b], B_con], writes=[BK[pb]])

        def s3(i):
            b, pb = i % 2, i % 2
            P.op(ACT, (lambda e: e.activation(out=hT[b], in_=bkb[pb], func=AF.Copy)), reads=[BK[pb]], writes=[BhT[b]])

        def s4(i):
            b, zb = i % 2, 2 + i % 2
            def mmz(e):
                ins = None
                for k in range(8):
                    ins = e.matmul(bkf[zb], lhsT=hT[b][:, k * 128:(k + 1) * 128], rhs=w_in_sl(k, 0, FW), start=(k == 0), stop=(k == 7))
                return ins
            P.op(PE, mmz, reads=[BhT[b]], writes=[BK[zb]])

        def s5(i):
            b, zb = i % 2, 2 + i % 2
            P.op(DVE, (lambda e: e.tensor_copy(out=zt[b], in_=bkf[zb])), reads=[BK[zb]], writes=[Bzt[b]])

        def s6(i):
            b, ab = i % 2, 4 + 2 * (i % 2)
            tc0 = i * 2 * KH
            P.op(PE, (lambda e: e.matmul(bkf[ab][0:KH, :], lhsT=tball[:, tc0:tc0 + KH], rhs=zt[b], start=True, stop=True)), reads=[B_tb, Bzt[b]], writes=[BK[ab]])
            P.op(PE, (lambda e: e.matmul(bkf[ab + 1][0:KH, :], lhsT=tball[:, tc0 + KH:tc0 + 2 * KH], rhs=zt[b], start=True, stop=True)), reads=[B_tb, Bzt[b]], writes=[BK[ab + 1]])

        def s7(i):
            b, ab = i % NA, 4 + 2 * (i % 2)
            At4 = At[b].rearrange("p (g t c) -> p g t c", g=4, t=2)
            P.op(ACT, (lambda e: e.activation(out=At4[0:KH, :, 0, :], in_=bkf[ab][0:KH, :].rearrange("p (g c) -> p g c", g=4), func=AF.Copy)), reads=[BK[ab]], writes=[BAt[b]])
            P.op(DVE, (lambda e: e.tensor_copy(out=At4[0:KH, :, 1, :], in_=bkf[ab + 1][0:KH, :].rearrange("p (g c) -> p g c", g=4))), reads=[BK[ab + 1]], writes=[BAt[b]])
            P.dma(POOL, (lambda e: e.dma_start(out=Ascr[:, i, :], in_=At[b][0:KH, :])), reads=[BAt[b]], writes=[B_A])

        nop = lambda i: None
        wstg = [WRK.alloc(D, F32) for _ in range(2)]; Bwstg = [Buf(), Buf()]
        B_wo = Buf("w_out_b")

        def wo_load(i):
            if bg_w_out and i < 8:
                P.dma(SP, (lambda e: e.dma_start(out=wstg[i % 2], in_=w_out[i * 128:(i + 1) * 128, :])), writes=[Bwstg[i % 2]])

        def wo_cast(i):
            if bg_w_out and i < 8:
                P.op(DVE, (lambda e: e.tensor_copy(out=w_out_b[:, i * D:(i + 1) * D], in_=wstg[i % 2])), reads=[Bwstg[i % 2]], writes=[B_wo])

        run_pipeline(NF, [s0, nop, nop, s1, s1b, s1c, s2, s3, s4, s5, s6, s7, wo_load, wo_cast])
        P.barrier()
        REG.reset(mk_r); WRK.reset(mk_w)

    def phase1b(NF, NKS, ntok, tabC, tabM, cs, Ascr, FTscr, NG):
        mk_r, mk_w = REG.mark(), WRK.mark()
        tC = REG.alloc(4 * NKS); tM = REG.alloc(4 * NKS)
        B_tC = Buf()
        P.dma(SP, lambda e: e.dma_start(out=tC[0:NF, :], in_=tabC.rearrange("s t n -> s (t n)")), writes=[B_tC])
        P.dma(SP, lambda e: e.dma_start(out=tM[0:NF, :], in_=tabM.rearrange("s t n -> s (t n)")), writes=[B_tC])
        XT = [REG.alloc(2 * ntok) for _ in range(NG)]
        B_XT = [Buf() for _ in range(NG)]
        NB = 3
        XW = NG * 256
        Ain = [REG.alloc(4 * XW) for _ in range(NB)]
        BAin = [Buf() for _ in range(NB)]
        FTo = [REG.alloc(512) for _ in range(2)]
        BFTo = [Buf() for _ in range(2)]
        B_FT = Buf("FTscr")
        Av = Ascr.rearrange("k s (gp x) -> k s gp x", gp=4 // NG)
        cpg = 8 * NKS
        gpb = max(1, min(NG, 512 // cpg))
        nbk = (NG + gpb - 1) // gpb
        assert nbk <= 2
        it = 0
        ld = 0
        ob = 0
        for gp in range(4 // NG):
            for kf0 in list(range(0, 64, 4)) + [64]:
                nk = 4 if kf0 < 64 else 1
                b = ld % NB
                ld += 1
                src = Av[kf0:kf0 + nk, :, gp, :].rearrange("k s x -> s k x")
                P.dma(SP, (lambda e, b=b, src=src, nk=nk: e.dma_start(out=Ain[b][0:NF, 0:nk * XW].rearrange("s (k x) -> s k x", k=nk), in_=src)), writes=[BAin[b]])
                for mirror in ((False, True) if kf0 < 64 else (False,)):
                    par = it % 2
                    it += 1
                    kls = [kl for kl in range(nk) if not (mirror and kf0 + kl == 0)]
                    tab = tM if mirror else tC
                    for bi in range(nbk):
                        pb = 2 * par + bi
                        gls = list(range(bi * gpb, min(NG, (bi + 1) * gpb)))
                        def mmc(e, b=b, pb=pb, gls=gls, kls=kls, tab=tab, mirror=mirror):
                            ins = None
                            for gi, gl in enumerate(gls):
                                for kl in kls:
                                    slot = (3 - kl) if mirror else kl
                                    o = bkf[pb][:, gi * cpg + slot * 2 * NKS:gi * cpg + (slot + 1) * 2 * NKS]
                                    base = kl * XW + gl * 256
                                    e.matmul(o, lhsT=Ain[b][0:NF, base:base + 128], rhs=tab[0:NF, 0:2 * NKS], start=True, stop=False)
                                    ins = e.matmul(o, lhsT=Ain[b][0:NF, base + 128:base + 256], rhs=tab[0:NF, 2 * NKS:4 * NKS], start=False, stop=True)
                            return ins
                        P.op(PE, mmc, reads=[BAin[b], B_tC], writes=[BK[pb]])
                        slots = sorted(((3 - kl) if mirror else kl) for kl in kls)
                        s0_, ns = slots[0], len(slots)
                        kdst0 = (128 - kf0 - 3 + s0_) if mirror else (kf0 + s0_)
                        for gi, gl in enumerate(gls):
                            src_v = bkf[pb][:, gi * cpg + s0_ * 2 * NKS:gi * cpg + (s0_ + ns) * 2 * NKS].rearrange("p (f r s) -> p f r s", f=ns, r=2)
                            dst_v = XT[gl].rearrange("p (r f s) -> p f r s", r=2, f=128, s=NKS)[:, kdst0:kdst0 + ns, :, :]
                            if bi == 0:
                                P.op(DVE, (lambda e, s_=src_v, d=dst_v: e.tensor_copy(out=d, in_=s_)), reads=[BK[pb]], writes=[B_XT[gl]])
                            else:
                                P.op(ACT, (lambda e, s_=src_v, d=dst_v: e.activation(out=d, in_=s_, func=AF.Copy)), reads=[BK[pb]], writes=[B_XT[gl]])
            for gl in range(NG):
                g = NG * gp + gl
                XTv = XT[gl].rearrange("p (r f s) -> p r s f", r=2, f=128, s=NKS)
                for c0 in range(0, ntok, 512):
                    c1 = min(ntok, c0 + 512)
                    n = c1 - c0
                    pb = 4 + ob % 2
                    fb = ob % 2
                    ob += 1
                    def mmch(e, c0=c0, n=n, pb=pb, XTv=XTv):
                        k0, nk_ = c0 // 128, n // 128
                        e.matmul(bkf[pb][:, 0:n], lhsT=cs[:, 0:128], rhs=XTv[:, 0, k0:k0 + nk_, :], start=True, stop=False)
                        return e.matmul(bkf[pb][:, 0:n], lhsT=cs[:, 128:256], rhs=XTv[:, 1, k0:k0 + nk_, :], start=False, stop=True)
                    P.op(PE, mmch, reads=[B_XT[gl], B_con], writes=[BK[pb]])
                    P.op(DVE, (lambda e, n=n, pb=pb, fb=fb: e.tensor_copy(out=FTo[fb][:, 0:n], in_=bkf[pb][:, 0:n])), reads=[BK[pb]], writes=[BFTo[fb]])
                    P.dma(POOL, (lambda e, g=g, c0=c0, n=n, fb=fb: e.dma_start(out=FTscr[g, :, c0:c0 + n], in_=FTo[fb][:, 0:n])), reads=[BFTo[fb]], writes=[B_FT])
        P.barrier()
        REG.reset(mk_r); WRK.reset(mk_w)

    phase1(xst, S_s, cfg.NF_s, tabA_s, A_s, bg_w_out=True)
    phase1(xpf, S_p, cfg.NF_p, tabA_p, A_p)
    phase1b(cfg.NF_s, cfg.NF_s, S_s, tabC_s, tabM_s, cs128[0], A_s, FT_s, 2)
    phase1b(cfg.NF_p, cfg.NKS_p, EXT, tabC_p, tabM_p, cs128[1], A_p, FT_p, 4)

    def mixer_pass(units):
        mk_r, mk_w = REG.mark(), WRK.mark()
        tiles = []
        for (xsrc, r0, FTscr, X1, x1r0, nt, mcol) in units:
            for t in range(nt):
                tiles.append((xsrc, r0 + t * 128, FTscr, X1, x1r0 + t * 128, mcol))
        NT = len(tiles)
        def ring(n, elems, dt=BF16):
            return [REG.alloc(elems, dt) for _ in range(n)], [Buf() for _ in range(n)]
        xn, Bxn = ring(3, D, F32)
        xr, Bxr = ring(3, D, F32)
        junk = REG.alloc(D); Bjunk = Buf()
        junk2 = REG.alloc(D, F32); Bjunk2 = Buf()
        Btmp2 = [Buf() for _ in range(3)]
        hb, Bhb = ring(2, D)
        hT, BhT = ring(2, D)
        uf, Buf_ = ring(5, 512)
        vf, Bvf = ring(3, 512, F32)
        sq, Bsq = ring(2, 512, F32)
        vn, Bvn = ring(2, 512)
        gt, Bgt = ring(2, 512)
        gT, BgT = ring(2, 512)
        ft, Bft = ring(3, 512)
        tmp, Btmp = ring(3, D, F32)
        NS = 20
        st, Bst = [WRK.alloc(16, F32) for _ in range(NS)], [Buf() for _ in range(NS)]
        B_X1 = Buf("X1scr")
        bA = (0, 7)
        bU, bV, bS, bG, bO = 1, 2, 3, 4, (5, 6)

        def s_load(i):
            xsrc, r0, FTscr, X1, x1r0, mcol = tiles[i]
            P.dma(SP, (lambda e: e.dma_start(out=xn[i % 3], in_=xsrc[r0:r0 + 128, :])), writes=[Bxn[i % 3]])

        def s_sq(i):
            s_ = st[i % NS]
            P.op(ACT, (lambda e: e.activation(out=junk, in_=xn[i % 3], func=AF.Square, accum_out=s_[:, 0:1])), reads=[Bxn[i % 3]], writes=[Bjunk, Bst[i % NS]])
            rstd_pool(s_[:, 0:1], s_[:, 1:2], 1, 128, 1.0 / D, Bst[i % NS])

        def s_scale(i):
            s_ = st[i % NS]
            P.op(ACT, (lambda e: e.activation(out=hb[i % 2], in_=xn[i % 3], func=AF.Copy, scale=s_[:, 1:2])), reads=[Bxn[i % 3], Bst[i % NS]], writes=[Bhb[i % 2]])

        def s_tr(i):
            pb = bA[i % 2]
            def tr(e):
                ins = None
                for k in range(8):
                    ins = e.transpose(out=bkb[pb][:, k * 128:(k + 1) * 128], in_=hb[i % 2][:, k * 128:(k + 1) * 128], identity=identb)
                return ins
            P.op(PE, tr, reads=[Bhb[i % 2], B_con], writes=[BK[pb]])

        def s_evh(i):
            pb = bA[i % 2]
            P.op(ACT, (lambda e: e.activation(out=hT[i % 2], in_=bkb[pb], func=AF.Copy)), reads=[BK[pb]], writes=[BhT[i % 2]])

        def s_uv(i):
            for (bank, c0) in ((bU, FW), (bV, 1024)):
                def mm(e, bank=bank, c0=c0):
                    ins = None
                    for k in range(8):
                        ins = e.matmul(bkf[bank], lhsT=hT[i % 2][:, k * 128:(k + 1) * 128], rhs=w_in_sl(k, c0, c0 + 512), start=(k == 0), stop=(k == 7))
                    return ins
                P.op(PE, mm, reads=[BhT[i % 2]], writes=[BK[bank]])

        def s_gelu(i):
            P.op(ACT, (lambda e: e.activation(out=vf[i % 3], in_=bkf[bV], func=AF.Gelu_apprx_tanh)), reads=[BK[bV]], writes=[Bvf[i % 3]])
            P.op(ACT, (lambda e: e.activation(out=uf[i % 5], in_=bkf[bU], func=AF.Gelu_apprx_tanh)), reads=[BK[bU]], writes=[Buf_[i % 5]])

        def s_vsq(i):
            s_ = st[i % NS]
            P.op(ACT, (lambda e: e.activation(out=sq[i % 2], in_=vf[i % 3], func=AF.Square)), reads=[Bvf[i % 3]], writes=[Bsq[i % 2]])
            P.op(DVE, (lambda e: e.tensor_reduce(out=s_[:, 4:8], in_=sq[i % 2].rearrange("p (h n) -> p h n", h=4), axis=mybir.AxisListType.X, op=ALU.add)), reads=[Bsq[i % 2]], writes=[Bst[i % NS]])
            rstd_pool(s_[:, 4:8], s_[:, 8:12], 4, 128, 1.0 / 128, Bst[i % NS])

        def s_vn(i):
            s_ = st[i % NS]
            for hd in range(4):
                P.op(DVE, (lambda e, hd=hd: e.scalar_tensor_tensor(out=vn[i % 2][:, hd * 128:(hd + 1) * 128], in0=vf[i % 3][:, hd * 128:(hd + 1) * 128], scalar=s_[:, 8 + hd:9 + hd], in1=sgain_bc[:, hd * 128:(hd + 1) * 128], op0=ALU.mult, op1=ALU.mult)), reads=[Bvf[i % 3], Bst[i % NS], B_con], writes=[Bvn[i % 2]])

        def s_sp(i):
            def mms(e):
                ins = None
                for hd in range(4):
                    ins = e.matmul(bkf[bS][:, hd * 128:(hd + 1) * 128], lhsT=WsT[:, hd * 128:(hd + 1) * 128], rhs=vn[i % 2][:, hd * 128:(hd + 1) * 128], start=True, stop=True)
                return ins
            P.op(PE, mms, reads=[Bvn[i % 2], B_con], writes=[BK[bS]])

        def s_gate(i):
            for hd in range(4):
                P.op(DVE, (lambda e, hd=hd: e.scalar_tensor_tensor(out=gt[i % 2][:, hd * 128:(hd + 1) * 128], in0=bkf[bS][:, hd * 128:(hd + 1) * 128], scalar=bcol[:, hd:hd + 1], in1=uf[i % 5][:, hd * 128:(hd + 1) * 128], op0=ALU.add, op1=ALU.mult)), reads=[BK[bS], Buf_[i % 5], B_con], writes=[Bgt[i % 2]])

        def s_trg(i):
            xsrc, r0, FTscr, X1, x1r0, mcol = tiles[i]
            P.dma(SP, (lambda e: e.dma_start(out=ft[i % 3].rearrange("p (g n) -> p g n", g=4), in_=FTscr[:, :, r0:r0 + 128].rearrange("g m n -> m g n"))), writes=[Bft[i % 3]])
            def tr(e):
                ins = None
                for hd in range(4):
                    ins = e.transpose(out=bkb[bG][:, hd * 128:(hd + 1) * 128], in_=gt[i % 2][:, hd * 128:(hd + 1) * 128], identity=identb)
                return ins
            P.op(PE, tr, reads=[Bgt[i % 2], B_con], writes=[BK[bG]])

        def s_evg(i):
            P.op(ACT, (lambda e: e.activation(out=gT[i % 2], in_=bkb[bG][:, 0:512], func=AF.Copy)), reads=[BK[bG]], writes=[BgT[i % 2]])

        def s_wo(i):
            def mmo(e):
                ins = None
                for nh in range(2):
                    for k in range(8):
                        l = ft[i % 3][:, k * 128:(k + 1) * 128] if k < 4 else gT[i % 2][:, (k - 4) * 128:(k - 3) * 128]
                        ins = e.matmul(bkf[bO[nh]], lhsT=l, rhs=w_out_sl(k, nh * 512, (nh + 1) * 512), start=(k == 0), stop=(k == 7))
                return ins
            P.op(PE, mmo, reads=[Bft[i % 3], BgT[i % 2]], writes=[BK[bO[0]], BK[bO[1]]])

        def s_ocp(i):
            xsrc, r0, FTscr, X1, x1r0, mcol = tiles[i]
            ob = i % 3
            P.dma(SP, (lambda e: e.dma_start(out=xr[i % 3], in_=xsrc[r0:r0 + 128, :])), writes=[Bxr[i % 3]])
            P.op(ACT, (lambda e: e.activation(out=tmp[ob][:, 0:512], in_=bkf[bO[0]], func=AF.Copy)), reads=[BK[bO[0]]], writes=[Btmp[ob]])
            P.op(DVE, (lambda e: e.tensor_copy(out=tmp[ob][:, 512:1024], in_=bkf[bO[1]])), reads=[BK[bO[1]]], writes=[Btmp2[ob]])

        def s_osq(i):
            s_ = st[i % NS]
            ob = i % 3
            P.op(ACT, (lambda e: e.activation(out=junk2, in_=tmp[ob], func=AF.Square, accum_out=s_[:, 14:15])), reads=[Btmp[ob], Btmp2[ob]], writes=[Bjunk2, Bst[i % NS]])
            rstd_pool(s_[:, 14:15], s_[:, 15:16], 1, 128, 1.0 / D, Bst[i % NS])

        def s_out(i):
            xsrc, r0, FTscr, X1, x1r0, mcol = tiles[i]
            s_ = st[i % NS]
            ob = i % 3
            P.op(DVE, (lambda e: e.scalar_tensor_tensor(out=tmp[ob], in0=tmp[ob], scalar=s_[:, 15:16], in1=gpm_bc, op0=ALU.mult, op1=ALU.mult)), reads=[Btmp[ob], Btmp2[ob], Bst[i % NS], B_con], writes=[Btmp[ob], Btmp2[ob]])
            P.op(DVE, (lambda e: e.tensor_tensor(out=tmp[ob], in0=tmp[ob], in1=xr[i % 3], op=ALU.add)), reads=[Btmp[ob], Btmp2[ob], Bxr[i % 3]], writes=[Btmp[ob], Btmp2[ob]])
            if mcol is not None:
                P.op(DVE, (lambda e: e.tensor_scalar(out=tmp[ob], in0=tmp[ob], scalar1=maskt[:, mcol:mcol + 1], scalar2=None, op0=ALU.mult)), reads=[Btmp[ob], Btmp2[ob], B_con], writes=[Btmp[ob], Btmp2[ob]])
            P.dma(POOL, (lambda e: e.dma_start(out=X1[x1r0:x1r0 + 128, :], in_=tmp[ob])), reads=[Btmp[ob], Btmp2[ob]], writes=[B_X1])

        run_pipeline(NT, [s_load, s_sq, s_scale, s_tr, s_evh, s_uv, s_gelu, s_vsq, s_vn, s_sp, s_gate, s_trg, s_evg, s_wo, s_ocp, s_osq, s_out])
        P.barrier()
        REG.reset(mk_r); WRK.reset(mk_w)

    units = []
    for i in range(S_s // 512):
        units.append((xs, i * 512, FT_s, X1_s, 1 + i * 512, 4, None))
    units.append((xpo, 0, FT_p, X1_p, 0, 1, 0))
    for i in range(OWN // 512):
        units.append((xpo, 128 + i * 512, FT_p, X1_p, 128 + i * 512, 4, None))
    units.append((xpo, 128 + OWN, FT_p, X1_p, 128 + OWN, 1, 1))
    mixer_pass(units)

    REG.reset(0)
    w_up_b = REG.alloc(8 * 2 * DFF)
    w_dn_b = REG.alloc(NJ * D)
    load_weight(w_up_b, w_up, 8, 2 * DFF, scale_col0=8, stage_cols=1408)

    w_down_dram = w_down

    def ffn_pass(funits):
        xa = WRK.alloc(D, F32); Bxa = Buf()
        xr = [WRK.alloc(D, F32) for _ in range(2)]; Bxr = [Buf(), Buf()]
        otmp = WRK.alloc(D, F32); Bot = Buf(); Bot2 = Buf()
        mk_t = WRK.mark()
        trr = [WRK.alloc(256, F32) for _ in range(2)]; Btrr = [Buf(), Buf()]
        junk5 = WRK.ap[:, mk_t // 2:mk_t // 2 + 1024].bitcast(F32)
        hb = [WRK.alloc(D) for _ in range(2)]; Bhb = [Buf(), Buf()]
        h2T = WRK.alloc(8 * 514); Bh2T = [Buf() for _ in range(5)]
        pT = WRK.alloc(NJ * 512); BpT = [[Buf(), Buf()] for _ in range(NJ)]
        mk_j = WRK.mark()
        tg = [WRK.alloc(256, F32) for _ in range(2)]; Btg = [Buf(), Buf()]
        tu = [WRK.alloc(256, F32) for _ in range(2)]; Btu = [Buf(), Buf()]
        junkf = WRK.ap[:, mk_j // 2:mk_j // 2 + 2048].bitcast(F32)
        Bjunkf = Btg + Btu
        gg = [WRK.alloc(256) for _ in range(2)]; Bgg = [Buf(), Buf()]
        st = [WRK.alloc(8, F32) for _ in range(4)]; Bst = [Buf() for _ in range(4)]
        h2T3 = h2T.rearrange("p (k n) -> p k n", k=8)
        pT3 = pT.rearrange("p (j n) -> p j n", j=NJ)
        B_Y = Buf("yout")
        cnt = {"p": 0, "o": 0}
        B_wdn = [Buf() for _ in range(NJ)]

        def wdn_task(j):
            stg, Bs_ = ((otmp, [Bot, Bot2]), (xr[0], [Bxr[0]]), (xr[1], [Bxr[1]]))[j % 3]
            P.dma(SP, (lambda e: e.dma_start(out=stg, in_=w_down_dram[j * 128:(j + 1) * 128, :])), writes=Bs_)
            P.op(DVE, (lambda e: e.tensor_copy(out=w_dn_b[:, j * D:(j + 1) * D], in_=stg)), reads=Bs_, writes=[B_wdn[j]])

        pst = {}

        def prep_a(unit, t):
            X1, r0, ydst, y0 = unit
            q = cnt["p"]; cnt["p"] += 1
            hbi, s_, Bs, pb = q % 2, st[q % 4], Bst[q % 4], q % 2
            if t < 4:
                np_ = 128
                xsrc_, Bxs_ = xa, Bxa
                P.dma(SP, (lambda e: e.dma_start(out=xa, in_=X1[r0 + t * 128:r0 + (t + 1) * 128, :])), writes=[Bxa])
            else:
                np_ = 2
                xsrc_, Bxs_ = xr[1], Bxr[1]
                P.dma(SP, (lambda e: e.dma_start(out=xr[1][0:1, :], in_=X1[r0 - 1:r0, :])), writes=[Bxr[1]])
                P.dma(SP, (lambda e: e.dma_start(out=xr[1][1:2, :], in_=X1[r0 + 512:r0 + 513, :])), writes=[Bxr[1]])
            norm_to_bf16(xsrc_[0:np_, :], np_, junkf, s_[:, 0:1], s_[:, 1:2], hb[hbi], Bxs_, Bs, Bhb[hbi], Bjunkf)
            pst[(id(unit), t)] = (hbi, pb, np_)

        def prep_b(unit, t):
            hbi, pb, np_ = pst.pop((id(unit), t))
            def tr(e):
                ins = None
                for k in range(8):
                    ins = e.transpose(out=bkb[pb][:, k * np_:(k + 1) * np_], in_=hb[hbi][0:np_, k * 128:(k + 1) * 128], identity=identb[0:np_, 0:np_])
                return ins
            P.op(PE, tr, reads=[Bhb[hbi], B_con], writes=[BK[pb]])
            if t < 4:
                dst = h2T3[:, :, 1 + t * 128:1 + (t + 1) * 128]
                srcv = bkb[pb].rearrange("p (k n) -> p k n", k=8)
                P.op(ACT, (lambda e: e.activation(out=dst, in_=srcv, func=AF.Copy)), reads=[BK[pb]], writes=[Bh2T[t]])
            else:
                srcv = bkb[pb][:, 0:16].rearrange("p (k n) -> p k n", k=8)
                P.op(ACT, (lambda e: e.activation(out=h2T3[:, :, 0:1], in_=srcv[:, :, 0:1], func=AF.Copy)), reads=[BK[pb]], writes=[Bh2T[t]])
                P.op(ACT, (lambda e: e.activation(out=h2T3[:, :, 513:514], in_=srcv[:, :, 1:2], func=AF.Copy)), reads=[BK[pb]], writes=[Bh2T[t]])

        def w_up_phase(extra=None):
            def bk_of(c):
                return ((0, 1), (2, 3), (4, 5))[c % 3]

            def wu0(c):
                j, c0 = c % NJ, (c // NJ) * 256
                pg, pu = bk_of(c)
                def mmg(e):
                    ins = None
                    for k in range(8):
                        ins = e.matmul(bkf[pg][:, 0:258], lhsT=w_up_b[:, k * 2 * DFF + j * 128:k * 2 * DFF + (j + 1) * 128], rhs=h2T3[:, k, c0:c0 + 258], start=(k == 0), stop=(k == 7))
                    return ins
                def mmup(e):
                    ins = None
                    for k in range(8):
                        ins = e.matmul(bkf[pu][:, 0:258], lhsT=w_up_b[:, k * 2 * DFF + DFF + j * 128:k * 2 * DFF + DFF + (j + 1) * 128], rhs=h2T3[:, k, c0:c0 + 258], start=(k == 0), stop=(k == 7))
                    return ins
                P.op(PE, mmg, reads=Bh2T, writes=[BK[pg]])
                P.op(PE, mmup, reads=Bh2T, writes=[BK[pu]])

            def wu1(c):
                j, cb = c % NJ, c % 2
                pg, pu = bk_of(c)
                jg, ju = j, NJ + j
                cw = lambda t, jj: cwT[:, t * 44 + jj:t * 44 + jj + 1]
                P.op(ACT, (lambda e: e.activation(out=tg[cb], in_=bkf[pg][:, 1:257], func=AF.Identity, scale=cw(1, jg), bias=cbT[:, jg:jg + 1])), reads=[BK[pg], B_con], writes=[Btg[cb]])
                P.op(ACT, (lambda e: e.activation(out=tu[cb], in_=bkf[pu][:, 1:257], func=AF.Identity, scale=cw(1, ju), bias=cbT[:, ju:ju + 1])), reads=[BK[pu], B_con], writes=[Btu[cb]])
                P.op(ACT, (lambda e: e.activation(out=trr[cb], in_=bkf[pu][:, 2:258], func=AF.Copy, scale=cw(2, ju))), reads=[BK[pu], B_con], writes=[Btrr[cb]])
                P.op(DVE, (lambda e: e.scalar_tensor_tensor(out=tg[cb], in0=bkf[pg][:, 0:256], scalar=cw(0, jg), in1=tg[cb], op0=ALU.mult, op1=ALU.add)), reads=[BK[pg], B_con, Btg[cb]], writes=[Btg[cb]])
                P.op(DVE, (lambda e: e.scalar_tensor_tensor(out=tg[cb], in0=bkf[pg][:, 2:258], scalar=cw(2, jg), in1=tg[cb], op0=ALU.mult, op1=ALU.add)), reads=[BK[pg], B_con, Btg[cb]], writes=[Btg[cb]])
                P.op(DVE, (lambda e: e.scalar_tensor_tensor(out=tu[cb], in0=bkf[pu][:, 0:256], scalar=cw(0, ju), in1=tu[cb], op0=ALU.mult, op1=ALU.add)), reads=[BK[pu], B_con, Btu[cb]], writes=[Btu[cb]])

            def wu2(c):
                j, cb, half = c % NJ, c % 2, c // NJ
                c0 = half * 256
                P.op(ACT, (lambda e: e.activation(out=gg[cb], in_=tg[cb], func=AF.Gelu_apprx_tanh)), reads=[Btg[cb]], writes=[Bgg[cb]])
                P.op(POOL, (lambda e: e.tensor_tensor(out=trr[cb], in0=tu[cb], in1=trr[cb], op=ALU.add)), reads=[Btu[cb], Btrr[cb]], writes=[Btrr[cb]])
                P.op(POOL, (lambda e: e.tensor_tensor(out=pT3[:, j, c0:c0 + 256], in0=gg[cb], in1=trr[cb], op=ALU.mult)), reads=[Bgg[cb], Btrr[cb]], writes=[BpT[j][half]])

            n = 2 * NJ
            for it in range(n + 2):
                if extra is not None:
                    extra(it)
                if it < n:
                    wu0(it)
                if 0 <= it - 1 < n:
                    wu1(it - 1)
                if 0 <= it - 2 < n:
                    wu2(it - 2)

        def w_down(unit, t, nh):
            def mmd(e):
                ins = None
                for j in range(NJ):
                    ins = e.matmul(bkf[6 + nh], lhsT=pT3[:, j, t * 128:(t + 1) * 128], rhs=w_dn_b[:, j * D + nh * 512:j * D + (nh + 1) * 512], start=(j == 0), stop=(j == NJ - 1))
                return ins
            P.op(PE, mmd, reads=[BpT[j][t // 2] for j in range(NJ)] + B_wdn, writes=[BK[6 + nh]])

        def post_a(unit, t, nh):
            X1, r0, ydst, y0 = unit
            xb = t % 2
            if nh == 0:
                P.dma(SP, (lambda e: e.dma_start(out=xr[xb], in_=X1[r0 + t * 128:r0 + (t + 1) * 128, :])), writes=[Bxr[xb]])
                P.op(DVE, (lambda e: e.tensor_copy(out=otmp[:, 0:512], in_=bkf[6])), reads=[BK[6]], writes=[Bot])
            else:
                P.op(ACT, (lambda e: e.activation(out=otmp[:, 512:1024], in_=bkf[7], func=AF.Copy)), reads=[BK[7]], writes=[Bot2])

        def post_b(unit, t):
            X1, r0, ydst, y0 = unit
            q = cnt["o"]; cnt["o"] += 1
            s_, Bs = st[q % 4], Bst[q % 4]
            xb = t % 2
            P.op(ACT, (lambda e: e.activation(out=junkf, in_=otmp, func=AF.Square, accum_out=s_[:, 4:5])), reads=[Bot, Bot2], writes=Bjunkf + [Bs])
            rstd_pool(s_[:, 4:5], s_[:, 5:6], 1, 128, 1.0 / D, Bs)
            P.op(DVE, (lambda e: e.scalar_tensor_tensor(out=otmp, in0=otmp, scalar=s_[:, 5:6], in1=gpf_bc, op0=ALU.mult, op1=ALU.mult)), reads=[Bot, Bot2, Bs, B_con], writes=[Bot, Bot2])
            P.op(POOL, (lambda e: e.tensor_tensor(out=xr[xb], in0=otmp, in1=xr[xb], op=ALU.add)), reads=[Bot, Bot2, Bxr[xb]], writes=[Bxr[xb]])
            P.dma(POOL, (lambda e: e.dma_start(out=ydst[y0 + t * 128:y0 + (t + 1) * 128, :], in_=xr[xb])), reads=[Bxr[xb]], writes=[B_Y])

        for t in range(5):
            prep_a(funits[0], t)
            prep_b(funits[0], t)
        for ui, unit in enumerate(funits):
            nxt = funits[ui + 1] if ui + 1 < len(funits) else None
            if ui == 0:
                w_up_phase(extra=(lambda it: wdn_task(it // 2) if (it % 2 == 0 and it // 2 < NJ) else None))
            else:
                w_up_phase()
            if nxt is not None:
                prep_a(nxt, 4)
                prep_a(nxt, 0)
            for t in range(4):
                w_down(unit, t, 0)
                post_a(unit, t, 0)
                if nxt is not None and t == 3:
                    prep_b(nxt, 3)
                w_down(unit, t, 1)
                post_a(unit, t, 1)
                if nxt is not None and t < 3:
                    prep_b(nxt, t)
                    if t == 0:
                        prep_b(nxt, 4)
                    prep_a(nxt, t + 1)
                post_b(unit, t)
        P.barrier()

    funits = []
    for i in range(S_s // 512):
        funits.append((X1_s, 1 + i * 512, ys, i * 512))
    for i in range(OWN // 512):
        funits.append((X1_p, 128 + i * 512, yp, i * 512))
    ffn_pass(funits)
    P.emit()
    return nc


def _bf(a):
    return np.ascontiguousarray(a.astype(np.float32)).astype(ml_dtypes.bfloat16)


def _tables(cfg):
    out = {}
    for name, S, NF in (("s", cfg.S_s, cfg.NF_s), ("p", cfg.S_p, cfg.NF_p)):
        sf = np.arange(NF)[:, None, None]
        ss_ = np.arange(128)[None, :, None]
        kf = np.arange(128)[None, None, :]
        s = (NF * ss_ + sf).astype(np.int64)
        ph = ((kf * s) % S).astype(np.float64) * (2 * np.pi / S)
        tab = np.stack([np.cos(ph)[:, :, 0:KH], -np.sin(ph)[:, :, 0:KH]], axis=2)
        out["tabA_" + name] = _bf(np.ascontiguousarray(tab.transpose(1, 0, 2, 3)).reshape(128, NF * 2 * KH))
        c = np.arange(128)[:, None]
        m = np.arange(128)[None, :]
        ph2 = ((c * m) % 128).astype(np.float64) * (2 * np.pi / 128)
        sc = 1.0 / np.sqrt(S * 128.0)
        out["cs_" + name] = _bf(np.stack([np.cos(ph2) * sc, np.sin(ph2) * sc], axis=1))
    return out


def _tabC(NF, ks_vals, mirror=False):
    sf = np.arange(NF)[:, None]
    ks = np.asarray(ks_vals, dtype=np.int64)[None, :] + (1 if mirror else 0)
    ph = ((sf * ks) % NF).astype(np.float64) * (2 * np.pi / NF)
    c, s = np.cos(ph), np.sin(ph)
    if not mirror:
        t0 = np.concatenate([c, -s], axis=1)
        t1 = np.concatenate([s, c], axis=1)
    else:
        t0 = np.concatenate([c, -s], axis=1)
        t1 = np.concatenate([-s, -c], axis=1)
    return _bf(np.stack([t0, t1], axis=1))


_CACHE = {}


def kernel_impl(cfg, x_prompt, x_sample, pre_mix_norm, w_in, sgu_norm, w_spatial, b_spatial, w_out,
                post_mix_norm, pre_ffn_norm, w_up, conv_w, conv_b, w_down, post_ffn_norm):
    f = lambda a: np.ascontiguousarray(np.asarray(a, dtype=np.float32))
    x_prompt, x_sample = f(x_prompt), f(x_sample)
    S_s, S_p, OWN, EXT = cfg.S_s, cfg.S_p, cfg.OWN, cfg.EXT
    key = (S_s, S_p, cfg.debug)
    if key not in _CACHE:
        _CACHE[key] = build_program(cfg)
    nc = _CACHE[key]
    tabs = _tables(cfg)
    common = dict(
        xpf=np.ascontiguousarray(x_prompt[0].reshape(128, cfg.NF_p, D).transpose(1, 0, 2)).reshape(S_p, D),
        g_pre=np.ascontiguousarray(np.stack([f(pre_mix_norm)[0].reshape(8, 128), f(pre_ffn_norm)[0].reshape(8, 128)])),
        w_in=f(w_in)[0], sgain=f(sgu_norm)[0].reshape(512), w_sp=f(w_spatial)[0], b_sp=f(b_spatial)[0].reshape(512),
        w_out=f(w_out)[0], g_pm=f(post_mix_norm)[0], w_up=f(w_up)[0],
        conv_w=np.ascontiguousarray(f(conv_w)[0].reshape(3 * 44, 128)), conv_b=np.ascontiguousarray(f(conv_b)[0].reshape(44, 128)),
        w_down=f(w_down)[0], g_pf=f(post_ffn_norm)[0],
        identb=_bf(np.eye(128)), identf=np.eye(128, dtype=np.float32),
        tabA_s=tabs["tabA_s"], tabA_p=tabs["tabA_p"], cs_s=tabs["cs_s"], cs_p=tabs["cs_p"],
        tabC_s=_tabC(cfg.NF_s, np.arange(cfg.NF_s)), tabM_s=_tabC(cfg.NF_s, np.arange(cfg.NF_s), mirror=True),
    )
    in_maps = []
    for j in range(NCORES):
        xpo = np.zeros((EXT, D), np.float32)
        lo, hi = j * OWN - 128, (j + 1) * OWN + 128
        a, b = max(lo, 0), min(hi, S_p)
        xpo[a - lo:b - lo] = x_prompt[0, a:b]
        mask = np.zeros((128, 2), np.float32)
        mask[:, 0] = 1.0 if j > 0 else 0.0
        mask[:, 1] = 1.0 if j < NCORES - 1 else 0.0
        ks0 = j * (OWN // 128) - 1
        m = dict(common)
        m.update(xs=x_sample[j], xst=np.ascontiguousarray(x_sample[j].reshape(128, cfg.NF_s, D).transpose(1, 0, 2)).reshape(S_s, D), xpo=xpo, mask=mask, tabC_p=_tabC(cfg.NF_p, np.arange(ks0, ks0 + cfg.NKS_p)),
                 tabM_p=_tabC(cfg.NF_p, np.arange(ks0, ks0 + cfg.NKS_p), mirror=True))
        in_maps.append(m)
    res = run_bass_kernel_spmd(nc, in_maps, core_ids=list(range(NCORES)))
    y_prompt = np.concatenate([r["yp"] for r in res.results], axis=0)[None]
    y_sample = np.stack([r["ys"] for r in res.results], axis=0)
    if cfg.debug:
        return (y_prompt.astype(np.float32), y_sample.astype(np.float32)), res.results
    return (y_prompt.astype(np.float32), y_sample.astype(np.float32))


def kernel(**inputs):
    return kernel_impl(Cfg(), **inputs)
```
